# Optimizing a Trainium2 kernel written in Bass

```python
import math
import jax, jax.numpy as jnp
from jax import lax
import numpy as np

D_MODEL = 2048
BATCH = 2
SEQ = 8192
DEPTH = 4

SB_HEADS = 8
SB_HEAD_DIM = 128
SB_WIDTH = SB_HEADS * SB_HEAD_DIM
Q_BLOCK = 128
SSM_HEAD_DIM = 64
SSM_INNER = D_MODEL // 2
SSM_HEADS = SSM_INNER // SSM_HEAD_DIM
SSM_GROUPS = 2
SSM_STATE = 128
SSM_CONV = 4
SSM_CHUNK = 128
CONV_DIM = SSM_INNER + 2 * SSM_GROUPS * SSM_STATE
D_IN_EVEN = 3 * SB_WIDTH + SSM_INNER + CONV_DIM + SSM_HEADS
MIX_WIDTH = SB_WIDTH + SSM_INNER
DIL_HEAD_DIM = 128
DIL_WIDTH = D_MODEL
DIL_HEADS = DIL_WIDTH // DIL_HEAD_DIM
DIL_PATTERNS = ((128, 1), (512, 4), (2048, 16))
DIL_BLOCK = 128
D_FF = 5632
FFN_CONV = 3
EPS = 1e-6
N_EVEN = (DEPTH + 1) // 2
N_ODD = DEPTH // 2

kernel_name = 'hybrid_stickbreak_ssd_dilated_convffn'


def rms_norm(x, w):
    xf = x.astype(jnp.float32)
    y = xf * lax.rsqrt(jnp.mean(xf * xf, axis=-1, keepdims=True) + EPS)
    return (y * w.astype(jnp.float32)).astype(x.dtype)


def causal_dwconv(x, w, b):
    K, C = w.shape
    y = lax.conv_general_dilated(x, w[:, None, :].astype(x.dtype), window_strides=(1,),
                                 padding=[(K - 1, 0)], dimension_numbers=('NWC', 'WIO', 'NWC'),
                                 feature_group_count=C)
    return y + b.astype(x.dtype)


def stick_breaking_attention(q, k, v):
    Bsz, T, H, Dh = q.shape
    nb = T // Q_BLOCK
    scale = Dh ** -0.5
    qb = q.reshape(Bsz, nb, Q_BLOCK, H, Dh).transpose(1, 0, 2, 3, 4)
    key_pos = jnp.arange(T)

    def block(args):
        qi, i = args
        z = jnp.einsum('bqhd,bkhd->bhqk', qi, k) * scale
        q_pos = i * Q_BLOCK + jnp.arange(Q_BLOCK)
        mask = key_pos[None, :] < q_pos[:, None]
        log_beta = jax.nn.log_sigmoid(z)
        log_keep = jnp.where(mask, log_beta - z, 0.0)
        later = lax.cumsum(log_keep, axis=3, reverse=True) - log_keep
        w = jnp.where(mask, jnp.exp(log_beta + later), 0.0)
        return jnp.einsum('bhqk,bkhd->bqhd', w, v)

    out = lax.map(block, (qb, jnp.arange(nb)))
    return out.transpose(1, 0, 2, 3, 4).reshape(Bsz, T, H, Dh)


def ssd_scan(x, dt, a, b_in, c_in):
    Bsz, T, H, P = x.shape
    G, N = b_in.shape[2], b_in.shape[3]
    R = H // G
    L = SSM_CHUNK
    C = T // L
    xdt = (x * dt[..., None]).reshape(Bsz, C, L, G, R, P)
    a_cs = jnp.cumsum((dt * a).reshape(Bsz, C, L, G, R), axis=2)
    bc = b_in.reshape(Bsz, C, L, G, N)
    cc = c_in.reshape(Bsz, C, L, G, N)
    seg = a_cs[:, :, :, None] - a_cs[:, :, None]
    causal = jnp.tril(jnp.ones((L, L), bool))[:, :, None, None]
    decay = jnp.exp(jnp.where(causal, seg, -jnp.inf))
    scores = jnp.einsum('bclgn,bcsgn->bclsg', cc, bc)
    y_diag = jnp.einsum('bclsgr,bcsgrp->bclgrp', scores[..., None] * decay, xdt)
    decay_to_end = jnp.exp(a_cs[:, :, -1:] - a_cs)
    states = jnp.einsum('bclgn,bclgr,bclgrp->bcgrpn', bc, decay_to_end, xdt)
    chunk_decay = jnp.exp(a_cs[:, :, -1])

    def step(h, inp):
        s_c, d_c = inp
        return h * d_c[..., None, None] + s_c, h

    h0 = jnp.zeros((Bsz, G, R, P, N), jnp.float32)
    _, prev = lax.scan(step, h0, (jnp.moveaxis(states, 1, 0), jnp.moveaxis(chunk_decay, 1, 0)))
    prev = jnp.moveaxis(prev, 0, 1)
    y_off = jnp.einsum('bclgn,bcgrpn,bclgr->bclgrp', cc, prev, jnp.exp(a_cs))
    return (y_diag + y_off).reshape(Bsz, T, H, P)


def even_mixer(hn, w_in, conv_w, conv_b, dt_bias, a_log, d_skip, ssm_norm_w, w_out):
    Bsz, T, _ = hn.shape
    f32 = jnp.float32
    proj = hn @ w_in
    cuts = [SB_WIDTH, 2 * SB_WIDTH, 3 * SB_WIDTH, 3 * SB_WIDTH + SSM_INNER,
            3 * SB_WIDTH + SSM_INNER + CONV_DIM]
    q, k, v, z, xbc, dt = jnp.split(proj, cuts, axis=-1)
    heads = lambda t: t.astype(f32).reshape(Bsz, T, SB_HEADS, SB_HEAD_DIM)
    o_a = stick_breaking_attention(heads(q), heads(k), heads(v)).reshape(Bsz, T, SB_WIDTH)
    xbc = jax.nn.silu(causal_dwconv(xbc, conv_w, conv_b)).astype(f32)
    xs, b_in, c_in = jnp.split(xbc, [SSM_INNER, SSM_INNER + SSM_GROUPS * SSM_STATE], axis=-1)
    dt = jax.nn.softplus(dt.astype(f32) + dt_bias.astype(f32))
    a = -jnp.exp(a_log.astype(f32))
    xs = xs.reshape(Bsz, T, SSM_HEADS, SSM_HEAD_DIM)
    y = ssd_scan(xs, dt, a, b_in.reshape(Bsz, T, SSM_GROUPS, SSM_STATE),
                 c_in.reshape(Bsz, T, SSM_GROUPS, SSM_STATE))
    y = y + d_skip.astype(f32)[:, None] * xs
    y = y.reshape(Bsz, T, SSM_INNER) * jax.nn.silu(z.astype(f32))
    o_b = rms_norm(y, ssm_norm_w)
    o = jnp.concatenate([o_a, o_b], axis=-1).astype(hn.dtype)
    return o @ w_out


def dilated_branch(q, k, v, window, dilation):
    Bsz, T, H, Dh = q.shape
    span = dilation * DIL_BLOCK
    T_pad = -(-T // span) * span
    M = T_pad // dilation
    nb = M // DIL_BLOCK
    reach = window // dilation

    def to_sub(t):
        t = jnp.pad(t, ((0, 0), (0, T_pad - T), (0, 0), (0, 0)))
        t = t.reshape(Bsz, M, dilation, H, Dh).transpose(0, 2, 1, 3, 4)
        return t.reshape(Bsz, dilation, nb, DIL_BLOCK, H, Dh)

    def with_prev(t):
        prev = jnp.pad(t, ((0, 0), (0, 0), (1, 0), (0, 0), (0, 0), (0, 0)))[:, :, :-1]
        return jnp.concatenate([prev, t], axis=3)

    def from_sub(t):
        rest = t.shape[4:]
        t = jnp.moveaxis(t.reshape((Bsz, dilation, M) + rest), 1, 2)
        return t.reshape((Bsz, T_pad) + rest)[:, :T]

    qs = to_sub(q)
    kb, vb = with_prev(to_sub(k)), with_prev(to_sub(v))
    s = jnp.einsum('brnqhe,brnkhe->brnhqk', qs, kb) * (Dh ** -0.5)
    qi = jnp.arange(DIL_BLOCK)[:, None]
    kj = jnp.arange(2 * DIL_BLOCK)[None, :]
    dist = qi + DIL_BLOCK - kj
    band = (dist >= 0) & (dist <= reach)
    mask = band[None] & ((jnp.arange(nb)[:, None, None] > 0) | (kj >= DIL_BLOCK)[None])
    s = jnp.where(mask[None, None, :, None], s, -jnp.inf)
    m = jnp.max(s, axis=-1, keepdims=True)
    p = jnp.exp(s - m)
    den = jnp.sum(p, axis=-1)
    o = jnp.einsum('brnhqk,brnkhe->brnqhe', p, vb) / jnp.swapaxes(den, -1, -2)[..., None]
    lse = jnp.swapaxes(m[..., 0] + jnp.log(den), -1, -2)
    return from_sub(o), from_sub(lse)


def dilated_mixture_attention(q, k, v):
    outs, lses = zip(*[dilated_branch(q, k, v, w, d) for (w, d) in DIL_PATTERNS])
    wts = jax.nn.softmax(jnp.stack(lses, 0), axis=0)
    return jnp.einsum('gbth,gbthe->bthe', wts, jnp.stack(outs, 0))


def odd_mixer(hn, w_in, w_out):
    Bsz, T, _ = hn.shape
    proj = (hn @ w_in).astype(jnp.float32)
    q, k, v = [t.reshape(Bsz, T, DIL_HEADS, DIL_HEAD_DIM) for t in jnp.split(proj, 3, axis=-1)]
    o = dilated_mixture_attention(q, k, v).reshape(Bsz, T, DIL_WIDTH).astype(hn.dtype)
    return o @ w_out


def conv_ffn(hn, w_gate, w_up, conv_w, conv_b, w_down):
    g = causal_dwconv(hn @ w_gate, conv_w, conv_b)
    return (jax.nn.silu(g) * (hn @ w_up)) @ w_down


def setup_inputs(seed: int = 0) -> dict:
    key = jax.random.key(seed)
    ks = iter(jax.random.split(key, 32))
    nrm = lambda shape, scale: jax.random.normal(next(ks), shape, jnp.float32) * scale
    gain = lambda shape: 1.0 + 0.02 * jax.random.normal(next(ks), shape, jnp.float32)
    dt0 = jnp.exp(jax.random.uniform(next(ks), (N_EVEN, SSM_HEADS), jnp.float32,
                                     math.log(1e-3), math.log(1e-1)))
    return {
        'x': nrm((BATCH, SEQ, D_MODEL), 1.0),
        'mix_norm_w': gain((DEPTH, D_MODEL)),
        'ffn_norm_w': gain((DEPTH, D_MODEL)),
        'final_norm_w': gain((D_MODEL,)),
        'ev_w_in': nrm((N_EVEN, D_MODEL, D_IN_EVEN), D_MODEL ** -0.5),
        'ev_conv_w': nrm((N_EVEN, SSM_CONV, CONV_DIM), SSM_CONV ** -0.5),
        'ev_conv_b': nrm((N_EVEN, CONV_DIM), 0.01),
        'ev_dt_bias': dt0 + jnp.log(-jnp.expm1(-dt0)),
        'ev_a_log': jnp.log(jax.random.uniform(next(ks), (N_EVEN, SSM_HEADS), jnp.float32, 1.0, 16.0)),
        'ev_d_skip': 1.0 + 0.1 * jax.random.normal(next(ks), (N_EVEN, SSM_HEADS), jnp.float32),
        'ev_ssm_norm_w': gain((N_EVEN, SSM_INNER)),
        'ev_w_out': nrm((N_EVEN, MIX_WIDTH, D_MODEL), MIX_WIDTH ** -0.5),
        'od_w_in': nrm((N_ODD, D_MODEL, 3 * DIL_WIDTH), D_MODEL ** -0.5),
        'od_w_out': nrm((N_ODD, DIL_WIDTH, D_MODEL), DIL_WIDTH ** -0.5),
        'ffn_w_gate': nrm((DEPTH, D_MODEL, D_FF), D_MODEL ** -0.5),
        'ffn_w_up': nrm((DEPTH, D_MODEL, D_FF), D_MODEL ** -0.5),
        'ffn_conv_w': nrm((DEPTH, FFN_CONV, D_FF), FFN_CONV ** -0.5),
        'ffn_conv_b': nrm((DEPTH, D_FF), 0.01),
        'ffn_w_down': nrm((DEPTH, D_FF, D_MODEL), D_FF ** -0.5),
    }


def reference(x, mix_norm_w, ffn_norm_w, final_norm_w, ev_w_in, ev_conv_w, ev_conv_b,
              ev_dt_bias, ev_a_log, ev_d_skip, ev_ssm_norm_w, ev_w_out, od_w_in, od_w_out,
              ffn_w_gate, ffn_w_up, ffn_conv_w, ffn_conv_b, ffn_w_down):
    h = x
    for layer in range(DEPTH):
        hn = rms_norm(h, mix_norm_w[layer])
        i = layer // 2
        if layer % 2 == 0:
            mix = even_mixer(hn, ev_w_in[i], ev_conv_w[i], ev_conv_b[i], ev_dt_bias[i],
                             ev_a_log[i], ev_d_skip[i], ev_ssm_norm_w[i], ev_w_out[i])
        else:
            mix = odd_mixer(hn, od_w_in[i], od_w_out[i])
        h = h + mix.astype(h.dtype)
        f = conv_ffn(rms_norm(h, ffn_norm_w[layer]), ffn_w_gate[layer], ffn_w_up[layer],
                     ffn_conv_w[layer], ffn_conv_b[layer], ffn_w_down[layer])
        h = h + f.astype(h.dtype)
    return rms_norm(h, final_norm_w)
```

```python
import numpy as np
from contextlib import ExitStack
import concourse.bass as bass
import concourse.mybir as mybir
from concourse.bass_utils import run_bass_kernel_spmd

F32 = mybir.dt.float32
BF16 = mybir.dt.bfloat16
AF = mybir.ActivationFunctionType
ALU = mybir.AluOpType
AX = mybir.AxisListType

NCORES = 8
D = 2048
SEQ = 8192
BATCH = 2
NTOK = BATCH * SEQ
TPC = NTOK // NCORES
DFF = 5632
EPS = 1e-6
KC = D // 128


class Buf:
    __slots__ = ("t", "w", "r", "name")

    def __init__(self, t, name):
        self.t = t
        self.w = None
        self.r = {}
        self.name = name

    def __getitem__(self, idx):
        return self.t[idx]


class Prog:
    ENGS = ("tensor", "vector", "scalar", "gpsimd", "sync")
    NDSEM = 8

    def __init__(self):
        self.nc = bass.Bass("TRN2", target_bir_lowering=False)
        self.es = ExitStack()
        nc = self.nc
        self.sem = {}
        self.cnt = {}
        self.seen = {e: {} for e in self.ENGS}
        for e in self.ENGS:
            self.sem[e] = self.es.enter_context(nc.semaphore("s_" + e))
            self.cnt[e] = 0
        self.dnext = {}
        for iss in ("sync", "gpsimd", "scalar"):
            self.dnext[iss] = 0
            for k in range(self.NDSEM):
                key = "d_%s%d" % (iss, k)
                self.sem[key] = self.es.enter_context(nc.semaphore(key))
                self.cnt[key] = 0
        self.sem["cc"] = self.es.enter_context(nc.semaphore("cc"))
        self.cnt["cc"] = 0
        self.nbuf = 0
        self.n_ins = 0
        self.scope = None

    def allgather(self, src, dst, groups=None):
        prev = ("cc", self.cnt["cc"]) if self.cnt["cc"] > 0 else None
        self._waits("gpsimd", [src], [dst], extra=prev)
        ins = self.nc.gpsimd.collective_compute(
            "AllGather", ALU.bypass, replica_groups=groups or [list(range(NCORES))],
            ins=[src.t.opt()], outs=[dst.t.opt()])
        self.cnt["cc"] += 1
        c = self.cnt["cc"]
        ins.then_inc(self.sem["cc"], 1)
        self.n_ins += 1
        src.r["cc"] = c
        dst.w = ("cc", c)
        dst.r = {}

    def sbuf(self, shape, dtype, name=None):
        self.nbuf += 1
        name = "%s_%d" % (name or "sb", self.nbuf)
        t = (self.scope or self.es).enter_context(self.nc.sbuf_tensor(name, list(shape), dtype))
        return Buf(t, name)

    def psum(self, shape, dtype=F32, name=None):
        self.nbuf += 1
        name = "%s_%d" % (name or "ps", self.nbuf)
        t = (self.scope or self.es).enter_context(self.nc.psum_tensor(name, list(shape), dtype))
        return Buf(t, name)

    def barrier(self):
        for e in self.ENGS:
            eng = getattr(self.nc, e)
            seen = self.seen[e]
            for k, c in self.cnt.items():
                if c > 0 and seen.get(k, 0) < c:
                    eng.wait_ge(self.sem[k], c)
                    seen[k] = c
                    self.n_ins += 1

    def begin_phase(self):
        self.scope = ExitStack()

    def end_phase(self):
        self.barrier()
        self.scope.close()
        self.scope = None

    def dram(self, name, shape, dtype, kind="Internal"):
        t = self.nc.dram_tensor(name, list(shape), dtype, kind=kind)
        return Buf(t.ap(), name)

    def _waits(self, eng, reads, writes, extra=None):
        needs = {}
        seen = self.seen[eng]

        def need(p):
            if p is None:
                return
            k, c = p
            if k == eng and eng == "tensor":
                return
            if seen.get(k, 0) >= c:
                return
            if needs.get(k, 0) < c:
                needs[k] = c

        for b in reads:
            need(b.w)
        for b in writes:
            need(b.w)
            for k, c in b.r.items():
                need((k, c))
        if extra is not None:
            need(extra)
        e = getattr(self.nc, eng)
        for k, c in needs.items():
            e.wait_ge(self.sem[k], c)
            seen[k] = c
            self.n_ins += 1

    def op(self, eng, fn, reads=(), writes=()):
        self._waits(eng, reads, writes)
        ins = fn(getattr(self.nc, eng))
        self.cnt[eng] += 1
        c = self.cnt[eng]
        ins.then_inc(self.sem[eng], 1)
        self.n_ins += 1
        for b in reads:
            b.r[eng] = c
        for b in writes:
            b.w = (eng, c)
            b.r = {}
        return ins

    def dma(self, iss, out, in_, reads=(), writes=(), **kw):
        k = self.dnext[iss]
        self.dnext[iss] = (k + 1) % self.NDSEM
        key = "d_%s%d" % (iss, k)
        prev = (key, self.cnt[key]) if self.cnt[key] > 0 else None
        self._waits(iss, reads, writes, extra=prev)
        ins = getattr(self.nc, iss).dma_start(out=out, in_=in_, **kw)
        self.cnt[key] += 16
        c = self.cnt[key]
        ins.then_inc(self.sem[key], 16)
        self.n_ins += 1
        for b in reads:
            b.r[key] = c
        for b in writes:
            b.w = (key, c)
            b.r = {}
        return ins

    def finish(self):
        s = self.nc.sync
        for key, c in self.cnt.items():
            if (key.startswith("d_") or key == "cc") and c > 0:
                s.wait_ge(self.sem[key], c)
        for e in self.ENGS:
            if e != "sync" and self.cnt[e] > 0:
                s.wait_ge(self.sem[e], self.cnt[e])
        self.es.close()
        return self.nc


def make_identity(p, dtype=BF16):
    ident = p.sbuf([128, 128], dtype, "ident")
    p.op("gpsimd", lambda e: e.memset(ident[:], 0.0), writes=[ident])
    p.op("gpsimd", lambda e: e.affine_select(
        out=ident[:], in_=ident[:], compare_op=ALU.not_equal, fill=1.0,
        base=0, pattern=[[-1, 128]], channel_multiplier=1),
        reads=[ident], writes=[ident])
    return ident


class NormT:
    def __init__(self, p, ident, wbc, banks):
        self.p = p
        self.ident = ident
        self.wbc = wbc
        self.hbuf = [p.sbuf([128, D], F32, "hld") for _ in range(2)]
        self.sq = p.sbuf([128, D], BF16, "sqjunk")
        self.hn = [p.sbuf([128, D], BF16, "hn") for _ in range(2)]
        self.ss = [p.sbuf([128, 1], F32, "ss") for _ in range(2)]
        self.rstd = [p.sbuf([128, 1], F32, "rstd") for _ in range(2)]
        self.tp = banks
        self.i = 0

    def run(self, h_rows_ap, nrows, dst, dst_col, src_buf=None, rowscale=None):
        p = self.p
        i = self.i
        self.i += 1
        hb = self.hbuf[i % 2]
        ss, rstd, hn = self.ss[i % 2], self.rstd[i % 2], self.hn[i % 2]
        p.dma("sync", hb[0:nrows, :], h_rows_ap, reads=[src_buf] if src_buf else [], writes=[hb])
        if rowscale is not None:
            p.op("vector", lambda e: e.tensor_scalar(hb[0:nrows, :], hb[0:nrows, :], rowscale[0:nrows, 0:1], None, ALU.mult),
                 reads=[hb, rowscale], writes=[hb])
        p.op("scalar", lambda e: e.activation(out=self.sq[0:nrows, :], in_=hb[0:nrows, :],
                                               func=AF.Square, accum_out=ss[0:nrows, :]),
             reads=[hb], writes=[self.sq, ss])
        p.op("vector", lambda e: e.tensor_scalar(rstd[0:nrows, :], ss[0:nrows, :], 1.0 / D, EPS,
                                                 ALU.mult, ALU.add), reads=[ss], writes=[rstd])
        p.op("scalar", lambda e: e.sqrt(rstd[0:nrows, :], rstd[0:nrows, :]), reads=[rstd], writes=[rstd])
        p.op("vector", lambda e: e.reciprocal(rstd[0:nrows, :], rstd[0:nrows, :]), reads=[rstd], writes=[rstd])
        p.op("vector", lambda e: e.scalar_tensor_tensor(
            out=hn[0:nrows, :], in0=hb[0:nrows, :], scalar=rstd[0:nrows, :], in1=self.wbc[0:nrows, :],
            op0=ALU.mult, op1=ALU.mult), reads=[hb, rstd, self.wbc], writes=[hn])
        for half in range(2):
            tpb = self.tp[half]
            tp = tpb[:].bitcast(BF16).rearrange("p (j n) -> p j n", j=8)
            for j in range(8):
                kc = half * 8 + j
                p.op("tensor", lambda e, kc=kc, j=j: e.transpose(
                    tp[:, j, 0:nrows], hn[0:nrows, kc * 128:(kc + 1) * 128], self.ident[0:nrows, 0:nrows]),
                    reads=[hn, self.ident], writes=[tpb])
            if half == 0:
                p.op("scalar", lambda e: e.copy(
                    out=dst[:, 0:8, dst_col:dst_col + nrows], in_=tp[:, :, 0:nrows]),
                    reads=[tpb], writes=[dst])
            else:
                p.op("vector", lambda e: e.tensor_copy(
                    out=dst[:, 8:16, dst_col:dst_col + nrows], in_=tp[:, :, 0:nrows]),
                    reads=[tpb], writes=[dst])
        return hb


def alloc_banks(p):
    return [p.psum([128, 512], F32, "bank") for _ in range(8)]


def wview(w_ap, f0, fn):
    return w_ap.rearrange("(kc p) f -> p kc f", p=128)[:, :, f0:f0 + fn]


def ffn_phase(p, banks, ident, hm, halo, halo_ap, nwbc_d, wg, wu, cwT_d, wd, out, NT=512, FS=256, flag_ap=None):
    NFC = DFF // 128
    p.begin_phase()
    flag = None
    if flag_ap is not None:
        flag = p.sbuf([128, 1], F32, "flag")
        p.dma("sync", flag[:], flag_ap, writes=[flag])
    wbc = p.sbuf([128, D], F32, "wbc")
    p.dma("sync", wbc[:], nwbc_d, writes=[wbc])
    cwT = p.sbuf([128, NFC, 4], F32, "cwT")
    p.dma("sync", cwT[:], cwT_d, writes=[cwT])
    norm = NormT(p, ident, wbc, banks[6:8])
    hnT = p.sbuf([128, KC, NT + 2], BF16, "hnT")
    actT = p.sbuf([128, NFC, NT], BF16, "actT")
    nfl = FS // 128
    wgs = [p.sbuf([128, KC, FS], BF16, "wgs") for _ in range(2)]
    wus = [p.sbuf([128, KC, FS], BF16, "wus") for _ in range(2)]
    WDG = 4
    wds = [p.sbuf([128, WDG, 1024], BF16, "wds") for _ in range(2)]
    gs = [p.sbuf([128, NT + 2], F32, "gs") for _ in range(2)]
    acc = [p.sbuf([128, NT], F32, "acc") for _ in range(2)]
    sl = [p.sbuf([128, NT], F32, "sl") for _ in range(2)]
    res = [p.sbuf([128, 1024], F32, "res") for _ in range(2)]
    stg = [p.sbuf([128, 1024], F32, "stg") for _ in range(2)]
    nslab = 0
    nwd = 0
    nev = 0
    for tb in range(TPC // NT):
        t0 = tb * NT
        if tb == 0:
            norm.run(halo_ap, 2, hnT, 0, src_buf=halo, rowscale=flag)
        else:
            norm.run(hm[t0 - 2:t0, :], 2, hnT, 0, src_buf=hm)
        for tt in range(NT // 128):
            norm.run(hm[t0 + tt * 128:t0 + (tt + 1) * 128, :], 128, hnT, 2 + tt * 128, src_buf=hm)
        for fs in range(DFF // FS):
            sg, su = wgs[nslab % 2], wus[nslab % 2]
            nslab += 1
            p.dma("sync", sg[:], wview(wg.t, fs * FS, FS), reads=[wg], writes=[sg])
            p.dma("sync", su[:], wview(wu.t, fs * FS, FS), reads=[wu], writes=[su])
            for fl in range(nfl):
                fc = fs * nfl + fl
                par = fc % 2
                G, U, H = banks[par], banks[2 + par], banks[4 + par]
                for kc in range(KC):
                    p.op("tensor", lambda e, kc=kc: e.matmul(
                        G[:, 0:NT], sg[:, kc, fl * 128:(fl + 1) * 128], hnT[:, kc, 2:NT + 2],
                        start=(kc == 0), stop=(kc == KC - 1)), reads=[sg, hnT], writes=[G])
                for kc in range(KC):
                    p.op("tensor", lambda e, kc=kc: e.matmul(
                        H[:, 0:2], sg[:, kc, fl * 128:(fl + 1) * 128], hnT[:, kc, 0:2],
                        start=(kc == 0), stop=(kc == KC - 1)), reads=[sg, hnT], writes=[H])
                for kc in range(KC):
                    p.op("tensor", lambda e, kc=kc: e.matmul(
                        U[:, 0:NT], su[:, kc, fl * 128:(fl + 1) * 128], hnT[:, kc, 2:NT + 2],
                        start=(kc == 0), stop=(kc == KC - 1)), reads=[su, hnT], writes=[U])
                g, a, s = gs[par], acc[par], sl[par]
                p.op("scalar", lambda e: e.copy(out=g[:, 2:NT + 2], in_=G[:, 0:NT]), reads=[G], writes=[g])
                p.op("scalar", lambda e: e.copy(out=g[:, 0:2], in_=H[:, 0:2]), reads=[H], writes=[g])
                p.op("vector", lambda e: e.tensor_scalar(
                    a[:], g[:, 2:NT + 2], cwT[:, fc, 2:3], cwT[:, fc, 3:4], ALU.mult, ALU.add),
                    reads=[g, cwT], writes=[a])
                p.op("vector", lambda e: e.scalar_tensor_tensor(
                    out=a[:], in0=g[:, 1:NT + 1], scalar=cwT[:, fc, 1:2], in1=a[:], op0=ALU.mult, op1=ALU.add),
                    reads=[g, cwT, a], writes=[a])
                p.op("vector", lambda e: e.scalar_tensor_tensor(
                    out=a[:], in0=g[:, 0:NT], scalar=cwT[:, fc, 0:1], in1=a[:], op0=ALU.mult, op1=ALU.add),
                    reads=[g, cwT, a], writes=[a])
                p.op("scalar", lambda e: e.activation(out=s[:], in_=a[:], func=AF.Silu), reads=[a], writes=[s])
                p.op("vector", lambda e: e.tensor_tensor(actT[:, fc, :], s[:], U[:, 0:NT], ALU.mult),
                     reads=[s, U], writes=[actT])
        for dh in range(2):
            for fg in range(NFC // WDG):
                wdb = wds[nwd % 2]
                nwd += 1
                src = wd.t.rearrange("(fc p) d -> p fc d", p=128)[:, fg * WDG:(fg + 1) * WDG, dh * 1024:(dh + 1) * 1024]
                p.dma("sync", wdb[:], src, reads=[wd], writes=[wdb])
                for fl in range(WDG):
                    fc = fg * WDG + fl
                    for tt in range(NT // 128):
                        for dq in range(2):
                            bk = banks[tt * 2 + dq]
                            p.op("tensor", lambda e, tt=tt, dq=dq, bk=bk: e.matmul(
                                bk[:, :], actT[:, fc, tt * 128:(tt + 1) * 128], wdb[:, fl, dq * 512:(dq + 1) * 512],
                                start=(fc == 0), stop=(fc == NFC - 1)), reads=[actT, wdb], writes=[bk])
            for tt in range(NT // 128):
                r, st = res[nev % 2], stg[nev % 2]
                nev += 1
                rows = slice(t0 + tt * 128, t0 + (tt + 1) * 128)
                p.dma("sync", r[:], hm[rows, dh * 1024:(dh + 1) * 1024], reads=[hm], writes=[r])
                for dq in range(2):
                    p.op("vector", lambda e, dq=dq: e.tensor_tensor(
                        st[:, dq * 512:(dq + 1) * 512], banks[tt * 2 + dq][:, :], r[:, dq * 512:(dq + 1) * 512], ALU.add),
                        reads=[banks[tt * 2 + dq], r], writes=[st])
                p.dma("sync", out[rows, dh * 1024:(dh + 1) * 1024], st[:], reads=[st], writes=[out])
    p.end_phase()


def evac(p, i, out_ap, in_ap, reads, writes):
    if i % 2 == 0:
        p.op("scalar", lambda e: e.copy(out=out_ap, in_=in_ap), reads=reads, writes=writes)
    else:
        p.op("vector", lambda e: e.tensor_copy(out=out_ap, in_=in_ap), reads=reads, writes=writes)


def proj_phase(p, banks, ident, h, nwbc_d, W, specs):
    p.begin_phase()
    wbc = p.sbuf([128, D], F32, "wbc")
    p.dma("sync", wbc[:], nwbc_d, writes=[wbc])
    norm = NormT(p, ident, wbc, banks[6:8])
    hnT = [p.sbuf([128, KC, 512], BF16, "hnT") for _ in range(TPC // 512)]
    for tt in range(TPC // 128):
        norm.run(h[tt * 128:(tt + 1) * 128, :], 128, hnT[tt // 4], (tt % 4) * 128, src_buf=h)
    WS = 512
    wsl = [p.sbuf([128, KC, WS], BF16, "wsl") for _ in range(2)]
    stT = [p.sbuf([128, TPC], F32, "stT") for _ in range(2)]
    stN = [p.sbuf([128, 512], F32, "stN") for _ in range(3)]
    nsl = 0
    nb = 0
    nst = 0
    for (c0, ncols, orient, out, dt) in specs:
        for s0 in range(0, ncols, WS):
            sw = min(WS, ncols - s0)
            ws = wsl[nsl % 2]
            nsl += 1
            p.dma("sync", ws[:, :, 0:sw], wview(W.t, c0 + s0, sw), reads=[W], writes=[ws])
            if orient == 'T':
                for fl in range(sw // 128):
                    st = stT[nst % 2]
                    nst += 1
                    stv = st[:].bitcast(BF16)[:, 0:TPC] if dt == BF16 else st[:, 0:TPC]
                    for tb in range(TPC // 512):
                        bk = banks[nb % 6]
                        nb += 1
                        for kc in range(KC):
                            p.op("tensor", lambda e, kc=kc: e.matmul(
                                bk[:, :], ws[:, kc, fl * 128:(fl + 1) * 128], hnT[tb][:, kc, :],
                                start=(kc == 0), stop=(kc == KC - 1)), reads=[ws, hnT[tb]], writes=[bk])
                        evac(p, nb, stv[:, tb * 512:(tb + 1) * 512], bk[:, :], [bk], [st])
                    r0 = s0 + fl * 128
                    p.dma("sync", out[r0:r0 + 128, :], stv, reads=[st], writes=[out])
            else:
                for tt in range(TPC // 128):
                    bk = banks[nb % 6]
                    nb += 1
                    for kc in range(KC):
                        p.op("tensor", lambda e, kc=kc: e.matmul(
                            bk[:, 0:sw], hnT[tt // 4][:, kc, (tt % 4) * 128:(tt % 4 + 1) * 128], ws[:, kc, 0:sw],
                            start=(kc == 0), stop=(kc == KC - 1)), reads=[ws, hnT[tt // 4]], writes=[bk])
                    st = stN[nst % 3]
                    nst += 1
                    stv = st[:].bitcast(BF16)[:, 0:sw] if dt == BF16 else st[:, 0:sw]
                    evac(p, nb, stv, bk[:, 0:sw], [bk], [st])
                    p.dma("sync", out[tt * 128:(tt + 1) * 128, s0:s0 + sw], stv, reads=[st], writes=[out])
    p.end_phase()


def weight_prep(p, shard, full, rows, cols):
    sb = p.dram(shard.name + "_b", [rows, cols], BF16)
    P_ = max(d for d in range(1, 129) if rows % d == 0)
    J = rows // P_
    CH = max(1, min(J, 16384 // (cols * 2) if cols * 2 <= 16384 else 1))
    p.begin_phase()
    tl = [p.sbuf([128, 8192], BF16, "wcast") for _ in range(2)]
    srcv = shard.t.rearrange("(p j) c -> p j c", p=P_)
    dstv = sb.t.rearrange("(p j) c -> p j c", p=P_)
    i = 0
    per = max(1, 8192 // cols)
    for j0 in range(0, J, per):
        jn = min(per, J - j0)
        t = tl[i % 2]
        i += 1
        tv = t[0:P_, 0:jn * cols].rearrange("p (j c) -> p j c", c=cols)
        p.dma("gpsimd", tv, srcv[:, j0:j0 + jn, :], reads=[shard], writes=[t])
        p.dma("sync", dstv[:, j0:j0 + jn, :], tv, reads=[t], writes=[sb])
    p.end_phase()
    p.allgather(sb, full)


def make_masks_sb(p):
    m32, m16 = [], []
    for r in range(4):
        a = p.sbuf([128, 512], F32, "msk32")
        p.op("gpsimd", lambda e: e.memset(a[:], 1.0), writes=[a])
        p.op("gpsimd", lambda e, r=r: e.affine_select(
            out=a[:], in_=a[:], compare_op=ALU.is_gt, fill=0.0, base=-r * 128,
            pattern=[[1, 512]], channel_multiplier=-1), reads=[a], writes=[a])
        b = p.sbuf([128, 512], BF16, "msk16")
        p.op("gpsimd", lambda e: e.tensor_copy(out=b[:], in_=a[:]), reads=[a], writes=[b])
        m32.append(a)
        m16.append(b)
    nt32 = p.sbuf([128, 128], F32, "nt32")
    p.op("gpsimd", lambda e: e.memset(nt32[:], -1.0), writes=[nt32])
    p.op("gpsimd", lambda e: e.affine_select(
        out=nt32[:], in_=nt32[:], compare_op=ALU.is_gt, fill=0.0, base=0,
        pattern=[[-1, 128]], channel_multiplier=1), reads=[nt32], writes=[nt32])
    nt = p.sbuf([128, 128], BF16, "nt")
    p.op("gpsimd", lambda e: e.tensor_copy(out=nt[:], in_=nt32[:]), reads=[nt32], writes=[nt])
    nr = p.sbuf([128, 128], BF16, "nr")
    p.op("gpsimd", lambda e: e.tensor_scalar(nr[:], nt32[:], -1.0, -1.0, ALU.mult, ALU.add),
         reads=[nt32], writes=[nr])
    return m32, m16, nt, nr


def static_srcs(qT, kT, v, T):
    nblk = max(1, T // 2048)
    bl = T // nblk
    qf = lambda u, r: qT[u * 128:(u + 1) * 128, r * bl:(r + 1) * bl]
    kf = lambda u, r: kT[u * 128:(u + 1) * 128, r * bl:(r + 1) * bl]
    vf = lambda u, r: v[r * bl:(r + 1) * bl, u * 128:(u + 1) * 128]
    return qf, kf, vf, nblk


def sb_attn_phase(p, banks, qT, kT, v, oT, NU=2, T=SEQ, srcs=None):
    p.begin_phase()
    qf, kf, vf, nblk = srcs or static_srcs(qT, kT, v, T)
    bl = T // nblk
    scale = 128.0 ** -0.5
    m32, m16, nt, nr = make_masks_sb(p)
    NQ = T // 512
    NKB = T // 128
    qs, ks, vs = [], [], []
    for u in range(NU):
        q = p.sbuf([128, T], BF16, "qs")
        k = p.sbuf([128, T], BF16, "ks")
        vv = p.sbuf([128, NKB, 128], BF16, "vs")
        for r in range(nblk):
            p.dma("sync", q[:, r * bl:(r + 1) * bl], qf(u, r), reads=[qT], writes=[q])
            p.dma("sync", k[:, r * bl:(r + 1) * bl], kf(u, r), reads=[kT], writes=[k])
            p.dma("sync", vv[:, r * (bl // 128):(r + 1) * (bl // 128), :],
                  vf(u, r).rearrange("(kb p) f -> p kb f", p=128), reads=[v], writes=[vv])
        qs.append(q); ks.append(k); vs.append(vv)
    NB = 3
    e_t = [p.sbuf([128, 512], F32, "e_t") for _ in range(NB)]
    sp_t = [p.sbuf([128, 512], F32, "sp_t") for _ in range(NB)]
    spb_t = [p.sbuf([128, 512], BF16, "spb_t") for _ in range(NB)]
    lb_t = [p.sbuf([128, 512], F32, "lb_t") for _ in range(NB)]
    arg_t = [p.sbuf([128, 512], F32, "arg_t") for _ in range(NB)]
    w_t = [p.sbuf([128, 512], BF16, "w_t") for _ in range(NB)]
    ost = [p.sbuf([128, 512], BF16, "ost") for _ in range(2)]
    Sb = [banks[0], banks[1], banks[2]]
    Cb = [banks[3], banks[4]]
    Ob = [banks[5], banks[6]]
    tiles = []
    for qi in range(NQ):
        nkb = 4 * qi + 4
        for j in range(nkb):
            kb = nkb - 1 - j
            for u in range(NU):
                tiles.append((qi, kb, u, j == 0, kb == 0))
    n = len(tiles)
    nost = 0

    def st_S(i):
        qi, kb, u, first, last = tiles[i]
        S = Sb[i % 3]
        p.op("tensor", lambda e: e.matmul(S[:, :], ks[u][:, kb * 128:(kb + 1) * 128], qs[u][:, qi * 512:(qi + 1) * 512],
                                          start=True, stop=True), reads=[ks[u], qs[u]], writes=[S])

    def st_elem1(i):
        qi, kb, u, first, last = tiles[i]
        S = Sb[i % 3]
        b = i % NB
        p.op("scalar", lambda e: e.activation(out=e_t[b][:], in_=S[:, :], func=AF.Exp, scale=scale),
             reads=[S], writes=[e_t[b]])
        p.op("scalar", lambda e: e.activation(out=sp_t[b][:], in_=e_t[b][:], func=AF.Ln, bias=1.0, scale=1.0),
             reads=[e_t[b]], writes=[sp_t[b]])
        r = kb - 4 * qi
        if r >= 0:
            p.op("gpsimd", lambda e: e.tensor_tensor(spb_t[b][:], sp_t[b][:], m32[r][:], ALU.mult),
                 reads=[sp_t[b], m32[r]], writes=[spb_t[b]])
        else:
            p.op("gpsimd", lambda e: e.tensor_copy(out=spb_t[b][:], in_=sp_t[b][:]),
                 reads=[sp_t[b]], writes=[spb_t[b]])
        p.op("vector", lambda e: e.scalar_tensor_tensor(
            out=lb_t[b][:], in0=S[:, :], scalar=scale, in1=sp_t[b][:], op0=ALU.mult, op1=ALU.subtract),
            reads=[S, sp_t[b]], writes=[lb_t[b]])

    def st_L(i):
        qi, kb, u, first, last = tiles[i]
        C = Cb[u]
        b = i % NB
        p.op("tensor", lambda e: e.matmul(C[:, :], nt[:], spb_t[b][:], start=first, stop=False),
             reads=[nt, spb_t[b]], writes=[C])
        p.op("vector", lambda e: e.tensor_tensor(arg_t[b][:], C[:, :], lb_t[b][:], ALU.add),
             reads=[C, lb_t[b]], writes=[arg_t[b]])
        p.op("tensor", lambda e: e.matmul(C[:, :], nr[:], spb_t[b][:], start=False, stop=last),
             reads=[nr, spb_t[b]], writes=[C])

    def st_w(i):
        qi, kb, u, first, last = tiles[i]
        b = i % NB
        p.op("scalar", lambda e: e.activation(out=w_t[b][:], in_=arg_t[b][:], func=AF.Exp),
             reads=[arg_t[b]], writes=[w_t[b]])
        r = kb - 4 * qi
        if r >= 0:
            p.op("gpsimd", lambda e: e.tensor_tensor(w_t[b][:], w_t[b][:], m16[r][:], ALU.mult),
                 reads=[w_t[b], m16[r]], writes=[w_t[b]])

    def st_O(i):
        nonlocal nost
        qi, kb, u, first, last = tiles[i]
        O = Ob[u]
        b = i % NB
        p.op("tensor", lambda e: e.matmul(O[:, :], vs[u][:, kb, :], w_t[b][:], start=first, stop=last),
             reads=[vs[u], w_t[b]], writes=[O])
        if last:
            st = ost[nost % 2]
            nost += 1
            p.op("vector", lambda e: e.tensor_copy(out=st[:], in_=O[:, :]), reads=[O], writes=[st])
            p.dma("sync", oT[u * 128:(u + 1) * 128, qi * 512:(qi + 1) * 512], st[:], reads=[st], writes=[oT])

    for step in range(n + 2):
        if step < n:
            st_S(step)
        if 1 <= step <= n:
            st_elem1(step - 1)
            st_L(step - 1)
        if step >= 2:
            st_w(step - 2)
            st_O(step - 2)
    p.end_phase()


def tri_const(p, keep_op, val=1.0, dtype=F32, name="tri", cm=1, step=-1):
    a = p.sbuf([128, 128], F32, name)
    p.op("gpsimd", lambda e: e.memset(a[:], val), writes=[a])
    p.op("gpsimd", lambda e: e.affine_select(
        out=a[:], in_=a[:], compare_op=keep_op, fill=0.0, base=0,
        pattern=[[step, 128]], channel_multiplier=cm), reads=[a], writes=[a])
    if dtype == F32:
        return a
    b = p.sbuf([128, 128], dtype, name + "b")
    p.op("gpsimd", lambda e: e.tensor_copy(out=b[:], in_=a[:]), reads=[a], writes=[b])
    return b


def bc_last(ap2, n):
    return ap2.unsqueeze(2).broadcast_to([ap2.shape[0], ap2.shape[1], n])


def ssd_phase(p, banks, xT, dtl, cw4, dtb, alog, dsk, y, T=SEQ, xsrc=None, dt_parts=None, xdeps=None):
    p.begin_phase()
    NCH = T // 128
    SEGL = 2048 if T >= 2048 else T
    NSEG = T // SEGL
    ident32 = make_identity(p, F32)
    identb = make_identity(p, BF16)
    TRIU = tri_const(p, ALU.is_ge, name="triu", cm=-1, step=1)
    T1 = tri_const(p, ALU.is_gt, name="t1")
    ONES = p.sbuf([128, 128], F32, "ones")
    p.op("gpsimd", lambda e: e.memset(ONES[:], 1.0), writes=[ONES])
    cw = p.sbuf([128, 4, 5], F32, "cw")
    if isinstance(cw4, (list, tuple)):
        for i_, ap_ in enumerate(cw4):
            p.dma("sync", cw[:, i_, :], ap_, writes=[cw])
    else:
        p.dma("sync", cw[:], cw4, writes=[cw])
    dsk_t = p.sbuf([128, 256], F32, "dsk")
    p.dma("sync", dsk_t[:], dsk, writes=[dsk_t])
    dt = p.sbuf([128, NCH, 4], F32, "dt")
    if dt_parts is None:
        p.dma("sync", dt[:], dtl, writes=[dt])
    else:
        npart = len(dt_parts)
        for i_, (ap_, dep_) in enumerate(dt_parts):
            p.dma("sync", dt[:, i_ * (NCH // npart):(i_ + 1) * (NCH // npart), :], ap_, reads=[dep_], writes=[dt])
    if xsrc is None:
        xdeps = [xT]
        xsrc = lambda ti, sg, halo: (xT[ti * 128:(ti + 1) * 128, sg * SEGL - 3:sg * SEGL] if halo
                                     else xT[ti * 128:(ti + 1) * 128, sg * SEGL:(sg + 1) * SEGL])
    dtb_t = p.sbuf([128, 4], F32, "dtb")
    p.dma("sync", dtb_t[:], dtb, writes=[dtb_t])
    a_t = p.sbuf([128, 4], F32, "a_t")
    p.dma("sync", a_t[:], alog, writes=[a_t])
    p.op("scalar", lambda e: e.activation(out=a_t[:], in_=a_t[:], func=AF.Exp), reads=[a_t], writes=[a_t])
    p.op("vector", lambda e: e.tensor_scalar(a_t[:], a_t[:], -1.0, None, ALU.mult), reads=[a_t], writes=[a_t])
    bcast_c = lambda t2: t2[:].unsqueeze(1).broadcast_to([128, NCH, 4])
    p.op("vector", lambda e: e.tensor_tensor(dt[:], dt[:], bcast_c(dtb_t), ALU.add), reads=[dt, dtb_t], writes=[dt])
    p.op("scalar", lambda e: e.activation(out=dt[:], in_=dt[:], func=AF.Exp), reads=[dt], writes=[dt])
    p.op("scalar", lambda e: e.activation(out=dt[:], in_=dt[:], func=AF.Ln, bias=1.0, scale=1.0), reads=[dt], writes=[dt])
    dta = p.sbuf([128, NCH, 4], F32, "dta")
    p.op("vector", lambda e: e.tensor_tensor(dta[:], dt[:], bcast_c(a_t), ALU.mult), reads=[dt, a_t], writes=[dta])
    ea = p.sbuf([128, NCH, 4], F32, "ea")
    dte = p.sbuf([128, NCH, 4], F32, "dte")
    cd = p.sbuf([128, NCH, 4], F32, "cd")
    dtaf = dta[:].rearrange("p c h -> p (c h)")
    for (mat, dst, bk) in ((TRIU, ea, banks[0]), (T1, dte, banks[1]), (ONES, cd, banks[2])):
        w = NCH * 4
        p.op("tensor", lambda e, mat=mat, bk=bk: e.matmul(bk[:, 0:w], mat[:], dtaf, start=True, stop=True),
             reads=[mat, dta], writes=[bk])
        p.op("scalar", lambda e, dst=dst, bk=bk: e.activation(
            out=dst[:].rearrange("p c h -> p (c h)"), in_=bk[:, 0:w], func=AF.Exp), reads=[bk], writes=[dst])
    dtdte = p.sbuf([128, NCH, 4], F32, "dtdte")
    p.op("vector", lambda e: e.tensor_tensor(dtdte[:], dt[:], dte[:], ALU.mult), reads=[dt, dte], writes=[dtdte])
    xtok = p.sbuf([128, NCH, 256], F32, "xtok")
    Btok = p.sbuf([128, NCH, 128], BF16, "Btok")
    BT = p.sbuf([128, T], BF16, "BT")
    CT = p.sbuf([128, T], BF16, "CT")
    segb = [p.sbuf([128, 3 + SEGL], F32, "segb") for _ in range(2)]
    xc = [p.sbuf([128, SEGL], F32, "xc") for _ in range(2)]
    xcb = p.sbuf([128, SEGL], BF16, "xcb")
    ns = 0
    ntp = 0
    for ti in range(4):
        for sg in range(NSEG):
            sbf = segb[ns % 2]
            xo = xc[ns % 2]
            ns += 1
            c0 = sg * SEGL
            if sg == 0:
                p.op("gpsimd", lambda e: e.memset(sbf[:, 0:3], 0.0), writes=[sbf])
            else:
                p.dma("sync", sbf[:, 0:3], xsrc(ti, sg, True), reads=xdeps, writes=[sbf])
            p.dma("sync", sbf[:, 3:3 + SEGL], xsrc(ti, sg, False), reads=xdeps, writes=[sbf])
            p.op("vector", lambda e: e.tensor_scalar(xo[:], sbf[:, 3:3 + SEGL], cw[:, ti, 3:4], cw[:, ti, 4:5],
                                                     ALU.mult, ALU.add), reads=[sbf, cw], writes=[xo])
            for k in range(3):
                p.op("vector", lambda e, k=k: e.scalar_tensor_tensor(
                    out=xo[:], in0=sbf[:, k:k + SEGL], scalar=cw[:, ti, k:k + 1], in1=xo[:],
                    op0=ALU.mult, op1=ALU.add), reads=[sbf, cw, xo], writes=[xo])
            if ti < 2:
                p.op("scalar", lambda e: e.activation(out=xo[:], in_=xo[:], func=AF.Silu), reads=[xo], writes=[xo])
                for cc in range(SEGL // 128):
                    ch = c0 // 128 + cc
                    bk = banks[6 + ntp % 2]
                    ntp += 1
                    p.op("tensor", lambda e, cc=cc, bk=bk: e.transpose(
                        bk[:, 0:128], xo[:, cc * 128:(cc + 1) * 128], ident32[:]), reads=[xo, ident32], writes=[bk])
                    evac(p, ntp, xtok[:, ch, ti * 128:(ti + 1) * 128], bk[:, 0:128], [bk], [xtok])
            elif ti == 2:
                p.op("scalar", lambda e: e.activation(out=BT[:, c0:c0 + SEGL], in_=xo[:], func=AF.Silu),
                     reads=[xo], writes=[BT])
                for cc in range(SEGL // 128):
                    ch = c0 // 128 + cc
                    bk = banks[6 + ntp % 2]
                    ntp += 1
                    bkb = bk[:].bitcast(BF16)
                    p.op("tensor", lambda e, cc=cc, bkb=bkb: e.transpose(
                        bkb[:, 0:128], BT[:, c0 + cc * 128:c0 + (cc + 1) * 128], identb[:]),
                        reads=[BT, identb], writes=[bk])
                    evac(p, ntp, Btok[:, ch, :], bkb[:, 0:128], [bk], [Btok])
            else:
                p.op("scalar", lambda e: e.activation(out=CT[:, c0:c0 + SEGL], in_=xo[:], func=AF.Silu),
                     reads=[xo], writes=[CT])
    H = p.sbuf([128, 256], F32, "H")
    Hb = p.sbuf([128, 256], BF16, "Hb")
    p.op("vector", lambda e: e.memset(H[:], 0.0), writes=[H])
    p.op("vector", lambda e: e.memset(Hb[:], 0.0), writes=[Hb])
    R = [p.sbuf([128, 4, 128], F32, "R") for _ in range(2)]
    E = [p.sbuf([128, 4, 128], F32, "E") for _ in range(2)]
    SM = [p.sbuf([128, 128], F32, "SM") for _ in range(2)]
    M = [p.sbuf([128, 4, 128], BF16, "M") for _ in range(2)]
    xdt = [p.sbuf([128, 256], BF16, "xdt") for _ in range(2)]
    xdte = [p.sbuf([128, 256], BF16, "xdte") for _ in range(2)]
    t1 = [p.sbuf([128, 256], F32, "t1") for _ in range(2)]
    yo = [p.sbuf([128, 256], F32, "yo") for _ in range(2)]
    for c in range(NCH):
        b = c % 2
        SEG, SC, YB, YO, ST = banks[0 + b], banks[2], banks[3], banks[4], banks[5]
        cs = slice(c * 128, (c + 1) * 128)
        for h in range(4):
            p.op("gpsimd", lambda e, h=h: e.tensor_scalar(R[b][:, h, :], TRIU[:], dta[:, c, h:h + 1], None, ALU.mult),
                 reads=[TRIU, dta], writes=[R[b]])
        p.op("tensor", lambda e: e.matmul(SEG[:, :], T1[:], R[b][:].rearrange("p h l -> p (h l)"), start=True, stop=True),
             reads=[T1, R[b]], writes=[SEG])
        p.op("scalar", lambda e: e.activation(out=E[b][:].rearrange("p h l -> p (h l)"), in_=SEG[:, :], func=AF.Exp),
             reads=[SEG], writes=[E[b]])
        p.op("tensor", lambda e: e.matmul(SC[:, 0:128], BT[:, cs], CT[:, cs], start=True, stop=True),
             reads=[BT, CT], writes=[SC])
        p.op("vector", lambda e: e.tensor_tensor(SM[b][:], SC[:, 0:128], TRIU[:], ALU.mult),
             reads=[SC, TRIU], writes=[SM[b]])
        p.op("vector", lambda e: e.tensor_tensor(
            M[b][:], E[b][:], SM[b][:].unsqueeze(1).broadcast_to([128, 4, 128]), ALU.mult),
            reads=[E[b], SM[b]], writes=[M[b]])
        for h in range(4):
            hs = slice(h * 64, (h + 1) * 64)
            p.op("scalar", lambda e, h=h, hs=hs: e.activation(
                out=xdt[b][:, hs], in_=xtok[:, c, hs], func=AF.Copy, scale=dt[:, c, h:h + 1]),
                reads=[xtok, dt], writes=[xdt[b]])
            p.op("scalar", lambda e, h=h, hs=hs: e.activation(
                out=xdte[b][:, hs], in_=xtok[:, c, hs], func=AF.Copy, scale=dtdte[:, c, h:h + 1]),
                reads=[xtok, dtdte], writes=[xdte[b]])
        for h in range(4):
            hs = slice(h * 64, (h + 1) * 64)
            p.op("tensor", lambda e, h=h, hs=hs: e.matmul(YB[:, hs], M[b][:, h, :], xdt[b][:, hs], start=True, stop=True),
                 reads=[M[b], xdt[b]], writes=[YB])
        p.op("tensor", lambda e: e.matmul(YO[:, 0:256], CT[:, cs], Hb[:], start=True, stop=True),
             reads=[CT, Hb], writes=[YO])
        p.op("tensor", lambda e: e.matmul(ST[:, 0:256], Btok[:, c, :], xdte[b][:], start=True, stop=True),
             reads=[Btok, xdte[b]], writes=[ST])
        v3 = lambda ap: ap.rearrange("p (h q) -> p h q", h=4)
        p.op("vector", lambda e: e.tensor_tensor(v3(t1[b][:]), v3(YO[:, 0:256]), bc_last(ea[:, c, :], 64), ALU.mult),
             reads=[YO, ea], writes=[t1[b]])
        p.op("vector", lambda e: e.tensor_tensor(t1[b][:], t1[b][:], YB[:, 0:256], ALU.add),
             reads=[t1[b], YB], writes=[t1[b]])
        p.op("gpsimd", lambda e: e.tensor_tensor(yo[b][:], xtok[:, c, :], dsk_t[:], ALU.mult),
             reads=[xtok, dsk_t], writes=[yo[b]])
        p.op("vector", lambda e: e.tensor_tensor(yo[b][:], yo[b][:], t1[b][:], ALU.add),
             reads=[yo[b], t1[b]], writes=[yo[b]])
        p.dma("sync", y[cs, :], yo[b][:], reads=[yo[b]], writes=[y])
        p.op("vector", lambda e: e.tensor_tensor(v3(H[:]), v3(H[:]), bc_last(cd[:, c, :], 64), ALU.mult),
             reads=[H, cd], writes=[H])
        p.op("vector", lambda e: e.tensor_tensor(H[:], H[:], ST[:, 0:256], ALU.add), reads=[H, ST], writes=[H])
        p.op("scalar", lambda e: e.copy(out=Hb[:], in_=H[:]), reads=[H], writes=[Hb])
    p.end_phase()


DIL = (1, 4, 16)


def dilated_phase(p, banks, qT, kT, v, oT, NU=4, T=SEQ, srcs=None):
    p.begin_phase()
    qf, kf, vf, nblk = srcs or static_srcs(qT, kT, v, T)
    bl = T // nblk
    scale = 128.0 ** -0.5
    m32 = p.sbuf([128, 256], F32, "dm32")
    p.op("gpsimd", lambda e: e.memset(m32[:], 1.0), writes=[m32])
    p.op("gpsimd", lambda e: e.affine_select(out=m32[:, 0:128], in_=m32[:, 0:128], compare_op=ALU.is_ge, fill=0.0,
                                             base=0, pattern=[[1, 128]], channel_multiplier=-1),
         reads=[m32], writes=[m32])
    p.op("gpsimd", lambda e: e.affine_select(out=m32[:, 128:256], in_=m32[:, 128:256], compare_op=ALU.is_ge, fill=0.0,
                                             base=0, pattern=[[-1, 128]], channel_multiplier=1),
         reads=[m32], writes=[m32])
    mk = p.sbuf([128, 256], BF16, "dmk")
    p.op("gpsimd", lambda e: e.tensor_copy(out=mk[:], in_=m32[:]), reads=[m32], writes=[mk])
    ones = p.sbuf([128, 128], BF16, "onesb")
    p.op("gpsimd", lambda e: e.memset(ones[:], 1.0), writes=[ones])
    qs = p.sbuf([128, T], BF16, "dqs")
    ks = p.sbuf([128, T], BF16, "dks")
    Vr = [p.sbuf([128, T // 128, 128], BF16, "dVr") for _ in range(2)]
    ACCN = p.sbuf([128, T], F32, "accn")
    ACCD = p.sbuf([128, T], F32, "accd")
    Pb = [p.sbuf([128, 256], BF16, "dP") for _ in range(4)]
    obuf = p.sbuf([128, T], BF16, "dob")
    Sb = [banks[0], banks[1]]
    Nb = [banks[2], banks[3]]
    Db = [banks[4], banks[5]]
    nS = 0
    nP = 0
    nG = 0
    nV = 0
    for u in range(NU):
        for rr in range(nblk):
            p.dma("sync", qs[:, rr * bl:(rr + 1) * bl], qf(u, rr), reads=[qT], writes=[qs])
            p.dma("sync", ks[:, rr * bl:(rr + 1) * bl], kf(u, rr), reads=[kT], writes=[ks])
        for bi, r in enumerate(DIL):
            nb = T // (128 * r)
            vr = Vr[nV % 2]
            nV += 1
            nbb = nb // nblk if nb >= nblk else 1
            for rho in range(r):
                for rr in range(nblk if nb >= nblk else 1):
                    sv = vf(u, rr) if nb >= nblk else v[:, u * 128:(u + 1) * 128]
                    sv = sv.rearrange("(m r) f -> r m f", r=r)[rho].rearrange("(kb q) f -> q kb f", q=128)
                    p.dma("sync", vr[:, rho * nb + rr * nbb:rho * nb + (rr + 1) * nbb, :], sv, reads=[v], writes=[vr])
            qv = qs[:].rearrange("p (m r) -> p r m", r=r)
            kv = ks[:].rearrange("p (m r) -> p r m", r=r)
            an = ACCN[:].rearrange("p (m r) -> p r m", r=r)
            ad = ACCD[:].rearrange("p (m r) -> p r m", r=r)
            for rho in range(r):
                prevP = None
                for kb in range(nb):
                    nq = 256 if kb < nb - 1 else 128
                    S = Sb[nS % 2]
                    nS += 1
                    P = Pb[nP % 4]
                    nP += 1
                    p.op("tensor", lambda e: e.matmul(S[:, 0:nq], kv[:, rho, kb * 128:(kb + 1) * 128],
                                                      qv[:, rho, kb * 128:kb * 128 + nq], start=True, stop=True),
                         reads=[ks, qs], writes=[S])
                    p.op("scalar", lambda e: e.activation(out=P[:, 0:nq], in_=S[:, 0:nq], func=AF.Exp, scale=scale),
                         reads=[S], writes=[P])
                    p.op("gpsimd", lambda e: e.tensor_tensor(P[:, 0:nq], P[:, 0:nq], mk[:, 0:nq], ALU.mult),
                         reads=[P, mk], writes=[P])
                    g, j = kb // 4, kb % 4
                    NB_, DB_ = Nb[nG % 2], Db[nG % 2]
                    cs = slice(j * 128, (j + 1) * 128)
                    if prevP is not None:
                        p.op("tensor", lambda e: e.matmul(NB_[:, cs], vr[:, rho * nb + kb - 1, :], prevP[:, 128:256],
                                                          start=True, stop=False), reads=[vr, prevP], writes=[NB_])
                        p.op("tensor", lambda e: e.matmul(DB_[:, cs], ones[:], prevP[:, 128:256],
                                                          start=True, stop=False), reads=[ones, prevP], writes=[DB_])
                    p.op("tensor", lambda e: e.matmul(NB_[:, cs], vr[:, rho * nb + kb, :], P[:, 0:128],
                                                      start=(prevP is None), stop=True), reads=[vr, P], writes=[NB_])
                    p.op("tensor", lambda e: e.matmul(DB_[:, cs], ones[:], P[:, 0:128],
                                                      start=(prevP is None), stop=True), reads=[ones, P], writes=[DB_])
                    prevP = P
                    if j == 3 or kb == nb - 1:
                        gw = (j + 1) * 128
                        ts_ = slice(g * 512, g * 512 + gw)
                        if bi == 0:
                            p.op("vector", lambda e: e.tensor_copy(out=an[:, rho, ts_], in_=NB_[:, 0:gw]),
                                 reads=[NB_], writes=[ACCN])
                            p.op("vector", lambda e: e.tensor_copy(out=ad[:, rho, ts_], in_=DB_[:, 0:gw]),
                                 reads=[DB_], writes=[ACCD])
                        else:
                            p.op("vector", lambda e: e.tensor_tensor(an[:, rho, ts_], NB_[:, 0:gw], an[:, rho, ts_], ALU.add),
                                 reads=[NB_, ACCN], writes=[ACCN])
                            p.op("vector", lambda e: e.tensor_tensor(ad[:, rho, ts_], DB_[:, 0:gw], ad[:, rho, ts_], ALU.add),
                                 reads=[DB_, ACCD], writes=[ACCD])
                        nG += 1
        for hh in range(T // 2048):
            hs = slice(hh * 2048, (hh + 1) * 2048)
            p.op("vector", lambda e: e.reciprocal(ACCD[:, hs], ACCD[:, hs]), reads=[ACCD], writes=[ACCD])
            p.op("vector", lambda e: e.tensor_tensor(obuf[:, hs], ACCN[:, hs], ACCD[:, hs], ALU.mult),
                 reads=[ACCN, ACCD], writes=[obuf])
        p.dma("sync", oT[u * 128:(u + 1) * 128, :], obuf[:], reads=[obuf], writes=[oT])
    p.end_phase()


def outproj_phase(p, banks, ident, h, o_src, n_att, Wout, out, gate=None):
    p.begin_phase()
    Wsb = p.sbuf([128, KC, D], BF16, "Wout")
    for q4 in range(4):
        p.dma("sync", Wsb[:, q4 * 4:(q4 + 1) * 4, :], Wout.t.rearrange("(kc p) f -> p kc f", p=128)[:, q4 * 4:(q4 + 1) * 4, :],
              reads=[Wout], writes=[Wsb])
    oTt = [p.sbuf([128, KC, 128], BF16, "oTt") for _ in range(2)]
    hres = [p.sbuf([128, D], F32, "hres") for _ in range(2)]
    stg = [p.sbuf([128, D], F32, "ostg") for _ in range(2)]
    if gate is not None:
        y_src, zb, gnw_ap, gdeps = gate
        gnw = p.sbuf([128, 1024], F32, "gnw")
        p.dma("sync", gnw[:], gnw_ap, writes=[gnw])
        yt = [p.sbuf([128, 1024], F32, "yt") for _ in range(2)]
        zt = [p.sbuf([128, 1024], F32, "zt") for _ in range(2)]
        gb = [p.sbuf([128, 1024], BF16, "gb") for _ in range(2)]
        junk = p.sbuf([128, 1024], BF16, "gjunk")
        gss = [p.sbuf([128, 1], F32, "gss") for _ in range(2)]
    att_deps = o_src.deps
    for tt in range(TPC // 128):
        b = tt % 2
        ot = oTt[b]
        for kc in range(n_att):
            p.dma("sync", ot[:, kc, :], o_src(kc, tt), reads=att_deps, writes=[ot])
        if gate is not None:
            for hq in range(4):
                p.dma("sync", yt[b][:, hq * 256:(hq + 1) * 256], y_src(hq, tt), reads=gdeps, writes=[yt[b]])
            p.dma("sync", zt[b][:], zb[tt * 128:(tt + 1) * 128, :], reads=[zb], writes=[zt[b]])
            p.op("scalar", lambda e: e.activation(out=zt[b][:], in_=zt[b][:], func=AF.Silu), reads=[zt[b]], writes=[zt[b]])
            p.op("vector", lambda e: e.tensor_tensor(yt[b][:], yt[b][:], zt[b][:], ALU.mult), reads=[yt[b], zt[b]], writes=[yt[b]])
            p.op("scalar", lambda e: e.activation(out=junk[:], in_=yt[b][:], func=AF.Square, accum_out=gss[b][:]),
                 reads=[yt[b]], writes=[junk, gss[b]])
            p.op("vector", lambda e: e.tensor_scalar(gss[b][:], gss[b][:], 1.0 / 1024, EPS, ALU.mult, ALU.add),
                 reads=[gss[b]], writes=[gss[b]])
            p.op("scalar", lambda e: e.sqrt(gss[b][:], gss[b][:]), reads=[gss[b]], writes=[gss[b]])
            p.op("vector", lambda e: e.reciprocal(gss[b][:], gss[b][:]), reads=[gss[b]], writes=[gss[b]])
            p.op("vector", lambda e: e.scalar_tensor_tensor(out=gb[b][:], in0=yt[b][:], scalar=gss[b][:], in1=gnw[:],
                                                            op0=ALU.mult, op1=ALU.mult), reads=[yt[b], gss[b], gnw], writes=[gb[b]])
            tpb = banks[6 + b]
            tp = tpb[:].bitcast(BF16).rearrange("p (j n) -> p j n", j=8)
            for j in range(8):
                p.op("tensor", lambda e, j=j: e.transpose(tp[:, j, :], gb[b][:, j * 128:(j + 1) * 128], ident[:]),
                     reads=[gb[b], ident], writes=[tpb])
            p.op("vector", lambda e: e.tensor_copy(out=ot[:, 8:16, :], in_=tp[:, :, :]), reads=[tpb], writes=[ot])
        p.dma("sync", hres[b][:], h[tt * 128:(tt + 1) * 128, :], reads=[h], writes=[hres[b]])
        for dq in range(4):
            bk = banks[(tt * 4 + dq) % 6]
            for kc in range(KC):
                p.op("tensor", lambda e, kc=kc: e.matmul(bk[:, :], ot[:, kc, :], Wsb[:, kc, dq * 512:(dq + 1) * 512],
                                                         start=(kc == 0), stop=(kc == KC - 1)), reads=[ot, Wsb], writes=[bk])
            p.op("vector", lambda e, dq=dq: e.tensor_tensor(stg[b][:, dq * 512:(dq + 1) * 512], bk[:, :],
                                                            hres[b][:, dq * 512:(dq + 1) * 512], ALU.add),
                 reads=[bk, hres[b]], writes=[stg[b]])
        p.dma("sync", out[tt * 128:(tt + 1) * 128, :], stg[b][:], reads=[stg[b]], writes=[out])
    p.end_phase()


def final_norm_phase(p, h, nwbc_ap, out):
    p.begin_phase()
    wbc = p.sbuf([128, D], F32, "fwbc")
    p.dma("sync", wbc[:], nwbc_ap, writes=[wbc])
    hb = [p.sbuf([128, D], F32, "fh") for _ in range(2)]
    ob = [p.sbuf([128, D], F32, "fo") for _ in range(2)]
    junk = p.sbuf([128, D], BF16, "fjunk")
    ss = [p.sbuf([128, 1], F32, "fss") for _ in range(2)]
    for tt in range(TPC // 128):
        b = tt % 2
        p.dma("sync", hb[b][:], h[tt * 128:(tt + 1) * 128, :], reads=[h], writes=[hb[b]])
        p.op("scalar", lambda e: e.activation(out=junk[:], in_=hb[b][:], func=AF.Square, accum_out=ss[b][:]),
             reads=[hb[b]], writes=[junk, ss[b]])
        p.op("vector", lambda e: e.tensor_scalar(ss[b][:], ss[b][:], 1.0 / D, EPS, ALU.mult, ALU.add), reads=[ss[b]], writes=[ss[b]])
        p.op("scalar", lambda e: e.sqrt(ss[b][:], ss[b][:]), reads=[ss[b]], writes=[ss[b]])
        p.op("vector", lambda e: e.reciprocal(ss[b][:], ss[b][:]), reads=[ss[b]], writes=[ss[b]])
        p.op("vector", lambda e: e.scalar_tensor_tensor(out=ob[b][:], in0=hb[b][:], scalar=ss[b][:], in1=wbc[:],
                                                        op0=ALU.mult, op1=ALU.mult), reads=[hb[b], ss[b], wbc], writes=[ob[b]])
        p.dma("sync", out[tt * 128:(tt + 1) * 128, :], ob[b][:], reads=[ob[b]], writes=[out])
    p.end_phase()


G4 = [[0, 1, 2, 3], [4, 5, 6, 7]]
DIN_E = 5648
DIN_O = 6144


class _Src:
    def __init__(self, fn, deps):
        self.fn = fn
        self.deps = deps

    def __call__(self, *a):
        return self.fn(*a)


def build_fused(depth=4, dump=None, stop=None):
    p = Prog()
    nc = p.nc
    ext = lambda name, shape, dt=F32: p.dram(name, shape, dt, "ExternalInput")
    x = ext("x", [TPC, D])
    mixn = [ext("mixn%d" % l, [128, D]) for l in range(4)]
    ffnn = [ext("ffnn%d" % l, [128, D]) for l in range(4)]
    finn = ext("finn", [128, D])
    flag = ext("flag", [128, 1])
    sh_ev_in = [ext("ev_in%d" % i, [D // 8, DIN_E]) for i in range(2)]
    sh_ev_out = [ext("ev_out%d" % i, [D // 8, D]) for i in range(2)]
    sh_od_in = [ext("od_in%d" % i, [D // 8, DIN_O]) for i in range(2)]
    sh_od_out = [ext("od_out%d" % i, [D // 8, D]) for i in range(2)]
    sh_wg = [ext("wg%d" % l, [D // 8, DFF]) for l in range(4)]
    sh_wu = [ext("wu%d" % l, [D // 8, DFF]) for l in range(4)]
    sh_wd = [ext("wd%d" % l, [DFF // 8, D]) for l in range(4)]
    ffn_cw = [ext("ffn_cw%d" % l, [128, DFF // 128, 4]) for l in range(4)]
    ev_cw = [ext("ev_cw%d" % i, [128, 4, 5]) for i in range(2)]
    ev_dtb = [ext("ev_dtb%d" % i, [128, 4]) for i in range(2)]
    ev_alog = [ext("ev_alog%d" % i, [128, 4]) for i in range(2)]
    ev_dsk = [ext("ev_dsk%d" % i, [128, 256]) for i in range(2)]
    ev_gnw = [ext("ev_gnw%d" % i, [128, 1024]) for i in range(2)]
    out = p.dram("out", [TPC, D], F32, "ExternalOutput")

    banks = alloc_banks(p)
    ident = make_identity(p)

    def prep(shards, rows, cols, name):
        res = []
        for i, sh in enumerate(shards):
            full = p.dram("%s_full%d" % (name, i), [rows * 8, cols], BF16)
            weight_prep(p, sh, full, rows, cols)
            res.append(full)
        return res
    n_even = (depth + 1) // 2
    n_odd = depth // 2
    W_ev_in = prep(sh_ev_in[:n_even], D // 8, DIN_E, "ev_in")
    W_ev_out = prep(sh_ev_out[:n_even], D // 8, D, "ev_out")
    W_od_in = prep(sh_od_in[:n_odd], D // 8, DIN_O, "od_in")
    W_od_out = prep(sh_od_out[:n_odd], D // 8, D, "od_out")
    W_g = prep(sh_wg[:depth], D // 8, DFF, "wg")
    W_u = prep(sh_wu[:depth], D // 8, DFF, "wu")
    W_d = prep(sh_wd[:depth], DFF // 8, D, "wd")

    hbuf = [p.dram("hA", [TPC, D], F32), p.dram("hB", [TPC, D], F32)]
    hm = p.dram("hm", [TPC, D], F32)
    halo_s = p.dram("halo_s", [2, D], F32)
    halo_g = p.dram("halo_g", [16, D], F32)

    def gathered(name, rows, cols, dt):
        loc = p.dram(name, [rows, cols], dt)
        gpad = p.dram(name + "g", [9 * rows, cols], dt)
        gat = Buf(gpad.t[0:8 * rows, :], name + "g")
        return loc, gat, gpad
    qTe, qTeg, qTegp = gathered("qTe", 1024, TPC, BF16)
    kTe, kTeg, kTegp = gathered("kTe", 1024, TPC, BF16)
    ve, veg, _ = gathered("ve", TPC, 1024, BF16)
    ze = p.dram("ze", [TPC, 1024], F32)
    xbcT, xbcTg, xbcTgp = gathered("xbcT", 1536, TPC, F32)
    dte, dteg, _ = gathered("dte", TPC, 16, F32)
    oTe, oTeg, _ = gathered("oTe", 256, SEQ, BF16)
    ye, yeg, _ = gathered("ye", SEQ, 256, F32)
    qTo, qTog, qTogp = gathered("qTo", 2048, TPC, BF16)
    kTo, kTog, kTogp = gathered("kTo", 2048, TPC, BF16)
    vo, vog, _ = gathered("vo", TPC, 2048, BF16)
    oTo, oTog, _ = gathered("oTo", 512, SEQ, BF16)
    qsel = p.dram("qsel", [256, SEQ], BF16); ksel = p.dram("ksel", [256, SEQ], BF16)
    vsel = p.dram("vsel", [SEQ, 256], BF16)
    xsel = p.dram("xsel", [512, SEQ], F32); dtsel = p.dram("dtsel", [SEQ, 4], F32)
    osel = p.dram("osel", [1024, TPC], BF16); ysel = p.dram("ysel", [TPC, 1024], F32)
    qselo = p.dram("qselo", [512, SEQ], BF16); kselo = p.dram("kselo", [512, SEQ], BF16)
    vselo = p.dram("vselo", [SEQ, 512], BF16); oselo = p.dram("oselo", [2048, TPC], BF16)

    ds = bass.ds
    pS = nc.sync.partition_id()
    jS, bS = pS % 4, pS // 4
    pA = nc.scalar.partition_id()
    jA, bA = pA % 4, pA // 4

    def selT(iss, bv, gat, gpad, dst, F, f0, nf, dst_r0=0):
        win = gpad.t[ds(bv * (4 * F) + f0, 4 * F), :].rearrange("(r f) t -> f r t", r=4)[0:nf, :, :]
        dv = dst.t[dst_r0:dst_r0 + nf, :].rearrange("f (r t) -> f r t", r=4)
        p.dma(iss, dv, win, reads=[gat], writes=[dst])

    def _stop(tag, buf=None):
        if stop == tag:
            if buf is not None:
                p.dma("sync", out[0:buf.t.shape[0] if buf.t.shape[0] < TPC else TPC, :] if False else out[:, :], buf[:, :], reads=[buf], writes=[out])
            return True
        return False

    h = x
    if _stop("prep"):
        return p.finish(), p
    for l in range(depth):
        i = l // 2
        if l % 2 == 0:
            specs = [(0, 1024, 'T', qTe, BF16), (1024, 1024, 'T', kTe, BF16), (2048, 1024, 'N', ve, BF16),
                     (3072, 1024, 'N', ze, F32), (4096, 1536, 'T', xbcT, F32), (5632, 16, 'N', dte, F32)]
            proj_phase(p, banks, ident, h, mixn[l].t, W_ev_in[i], specs)
            if _stop("proj0"):
                return p.finish(), p
            for s_, g_ in ((qTe, qTeg), (kTe, kTeg), (ve, veg), (xbcT, xbcTg), (dte, dteg)):
                p.allgather(s_, g_)
            if _stop("ag0"):
                return p.finish(), p
            selT("sync", bS, qTeg, qTegp, qsel, 1024, jS * 256, 256)
            selT("sync", bS, kTeg, kTegp, ksel, 1024, jS * 256, 256)
            p.dma("sync", vsel[:, :], veg[ds(bS * SEQ, SEQ), ds(jS * 256, 256)], reads=[veg], writes=[vsel])
            selT("sync", bS, xbcTg, xbcTgp, xsel, 1536, jS * 256, 256, 0)
            selT("sync", bS, xbcTg, xbcTgp, xsel, 1536, 1024 + (jS // 2) * 128, 128, 256)
            selT("sync", bS, xbcTg, xbcTgp, xsel, 1536, 1280 + (jS // 2) * 128, 128, 384)
            p.dma("sync", dtsel[:, :], dteg[ds(bS * SEQ, SEQ), ds(jS * 4, 4)], reads=[dteg], writes=[dtsel])
            if _stop("sel0"):
                return p.finish(), p
            sb_attn_phase(p, banks, qsel, ksel, vsel, oTe, NU=2, T=SEQ)
            p.allgather(oTe, oTeg)
            if _stop("sb0"):
                return p.finish(), p
            ssd_phase(p, banks, xsel, dtsel.t.rearrange("(c q) h -> q c h", q=128), ev_cw[i].t, ev_dtb[i].t, ev_alog[i].t,
                      ev_dsk[i].t, ye, T=SEQ)
            p.allgather(ye, yeg)
            p.dma("sync", osel[:, :], oTeg[ds(bS * 1024, 1024), ds(jS * TPC, TPC)], reads=[oTeg], writes=[osel])
            p.dma("sync", ysel.t.rearrange("t (hq c) -> t hq c", hq=4),
                  yeg.t.rearrange("(hq t) c -> t hq c", hq=8)[ds(jS * TPC, TPC), ds(bS * 4, 4), :], reads=[yeg], writes=[ysel])
            o_src = _Src(lambda kc, tt: osel[kc * 128:(kc + 1) * 128, tt * 128:(tt + 1) * 128], [osel])
            y_src = lambda hq, tt: ysel[tt * 128:(tt + 1) * 128, hq * 256:(hq + 1) * 256]
            outproj_phase(p, banks, ident, h, o_src, 8, W_ev_out[i], hm, gate=(y_src, ze, ev_gnw[i].t, [ysel]))
        else:
            specs = [(0, 2048, 'T', qTo, BF16), (2048, 2048, 'T', kTo, BF16), (4096, 2048, 'N', vo, BF16)]
            proj_phase(p, banks, ident, h, mixn[l].t, W_od_in[i], specs)
            for s_, g_ in ((qTo, qTog), (kTo, kTog), (vo, vog)):
                p.allgather(s_, g_)
            selT("scalar", bA, qTog, qTogp, qselo, 2048, jA * 512, 512)
            selT("scalar", bA, kTog, kTogp, kselo, 2048, jA * 512, 512)
            p.dma("scalar", vselo[:, :], vog[ds(bA * SEQ, SEQ), ds(jA * 512, 512)], reads=[vog], writes=[vselo])
            dilated_phase(p, banks, qselo, kselo, vselo, oTo, NU=4, T=SEQ)
            p.allgather(oTo, oTog)
            p.dma("scalar", oselo[:, :], oTog[ds(bA * 2048, 2048), ds(jA * TPC, TPC)], reads=[oTog], writes=[oselo])
            o_src = _Src(lambda kc, tt: oselo[kc * 128:(kc + 1) * 128, tt * 128:(tt + 1) * 128], [oselo])
            outproj_phase(p, banks, ident, h, o_src, 16, W_od_out[i], hm)
        p.dma("sync", halo_s[:, :], hm[TPC - 2:TPC, :], reads=[hm], writes=[halo_s])
        p.allgather(halo_s, halo_g)
        hn = hbuf[l % 2]
        ffn_phase(p, banks, ident, hm, halo_g, halo_g[ds((bS * 4 + (jS + 3) % 4) * 2, 2), :], ffnn[l].t,
                  W_g[l], W_u[l], ffn_cw[l].t, W_d[l], hn, flag_ap=flag.t)
        h = hn
        if dump is not None and dump.get("after_layer") == l:
            break
    if dump is None:
        final_norm_phase(p, h, finn.t, out)
    else:
        p.dma("sync", out[:, :], h[:, :], reads=[h], writes=[out])
    return p.finish(), p


def _bc(v, n=128):
    return np.ascontiguousarray(np.broadcast_to(np.asarray(v, np.float32), (n,) + np.asarray(v).shape))


def make_in_maps(inp, depth=4):
    f = lambda a: np.ascontiguousarray(np.asarray(a, dtype=np.float32))
    xs = f(inp["x"]).reshape(NTOK, D)
    common = {}
    for l in range(4):
        common["mixn%d" % l] = _bc(inp["mix_norm_w"][l])
        common["ffnn%d" % l] = _bc(inp["ffn_norm_w"][l])
        cw = np.concatenate([f(inp["ffn_conv_w"][l]), f(inp["ffn_conv_b"][l])[None]], 0)
        common["ffn_cw%d" % l] = np.ascontiguousarray(cw.reshape(4, DFF // 128, 128).transpose(2, 1, 0))
    common["finn"] = _bc(inp["final_norm_w"])
    for i in range(2):
        cw = np.concatenate([f(inp["ev_conv_w"][i]), f(inp["ev_conv_b"][i])[None]], 0)
        cwl = np.ascontiguousarray(cw.reshape(5, 12, 128).transpose(2, 1, 0))
        for jj in range(4):
            tiles = [2 * jj, 2 * jj + 1, 8 + jj // 2, 10 + jj // 2]
            common["ev_cw%d_j%d" % (i, jj)] = np.ascontiguousarray(cwl[:, tiles, :])
            common["ev_dtb%d_j%d" % (i, jj)] = _bc(f(inp["ev_dt_bias"][i])[4 * jj:4 * jj + 4])
            common["ev_alog%d_j%d" % (i, jj)] = _bc(f(inp["ev_a_log"][i])[4 * jj:4 * jj + 4])
            common["ev_dsk%d_j%d" % (i, jj)] = _bc(np.repeat(f(inp["ev_d_skip"][i])[4 * jj:4 * jj + 4], 64))
        common["ev_gnw%d" % i] = _bc(inp["ev_ssm_norm_w"][i])
    maps = []
    for c in range(NCORES):
        m = {k_: v_ for k_, v_ in common.items() if "_j" not in k_}
        for i in range(2):
            for nm in ("ev_cw", "ev_dtb", "ev_alog", "ev_dsk"):
                m["%s%d" % (nm, i)] = common["%s%d_j%d" % (nm, i, c % 4)]
        m["x"] = xs[c * TPC:(c + 1) * TPC]
        m["flag"] = np.full((128, 1), 0.0 if c % 4 == 0 else 1.0, np.float32)
        r = slice(c * (D // 8), (c + 1) * (D // 8))
        rd = slice(c * (DFF // 8), (c + 1) * (DFF // 8))
        for i in range(2):
            m["ev_in%d" % i] = f(inp["ev_w_in"][i][r])
            m["ev_out%d" % i] = f(inp["ev_w_out"][i][r])
            m["od_in%d" % i] = f(inp["od_w_in"][i][r])
            m["od_out%d" % i] = f(inp["od_w_out"][i][r])
        for l in range(4):
            m["wg%d" % l] = f(inp["ffn_w_gate"][l][r])
            m["wu%d" % l] = f(inp["ffn_w_up"][l][r])
            m["wd%d" % l] = f(inp["ffn_w_down"][l][rd])
        maps.append(m)
    return maps


def kernel(**inputs):
    nc, _ = build_fused()
    maps = make_in_maps(inputs)
    res = run_bass_kernel_spmd(nc, maps, core_ids=list(range(NCORES)))
    outp = np.concatenate([res.results[c]["out"] for c in range(NCORES)], axis=0)
    return outp.reshape(BATCH, SEQ, D).astype(np.float32)
```

```python
import numpy as np
from contextlib import ExitStack
import concourse.bass as bass
import concourse.mybir as mybir
from concourse.bass_utils import run_bass_kernel_spmd

F32 = mybir.dt.float32
BF16 = mybir.dt.bfloat16
AF = mybir.ActivationFunctionType
ALU = mybir.AluOpType
AX = mybir.AxisListType

NCORES = 8
D = 2048
SEQ = 8192
BATCH = 2
NTOK = BATCH * SEQ
TPC = NTOK // NCORES
DFF = 5632
EPS = 1e-6
KC = D // 128


class Buf:
    __slots__ = ("t", "w", "r", "name")

    def __init__(self, t, name):
        self.t = t
        self.w = None
        self.r = {}
        self.name = name

    def __getitem__(self, idx):
        return self.t[idx]


class Prog:
    ENGS = ("tensor", "vector", "scalar", "gpsimd", "sync")
    NDSEM = 8

    def __init__(self):
        self.nc = bass.Bass("TRN2", target_bir_lowering=False)
        self.es = ExitStack()
        nc = self.nc
        self.sem = {}
        self.cnt = {}
        self.seen = {e: {} for e in self.ENGS}
        for e in self.ENGS:
            self.sem[e] = self.es.enter_context(nc.semaphore("s_" + e))
            self.cnt[e] = 0
        self.dnext = {}
        for iss in ("sync", "gpsimd", "scalar"):
            self.dnext[iss] = 0
            for k in range(self.NDSEM):
                key = "d_%s%d" % (iss, k)
                self.sem[key] = self.es.enter_context(nc.semaphore(key))
                self.cnt[key] = 0
        self.sem["cc"] = self.es.enter_context(nc.semaphore("cc"))
        self.cnt["cc"] = 0
        self.nbuf = 0
        self.n_ins = 0
        self.scope = None

    def allgather(self, src, dst, groups=None):
        prev = ("cc", self.cnt["cc"]) if self.cnt["cc"] > 0 else None
        self._waits("gpsimd", [src], [dst], extra=prev)
        ins = self.nc.gpsimd.collective_compute(
            "AllGather", ALU.bypass, replica_groups=groups or [list(range(NCORES))],
            ins=[src.t.opt()], outs=[dst.t.opt()])
        self.cnt["cc"] += 1
        c = self.cnt["cc"]
        ins.then_inc(self.sem["cc"], 1)
        self.n_ins += 1
        src.r["cc"] = c
        dst.w = ("cc", c)
        dst.r = {}

    def sbuf(self, shape, dtype, name=None):
        self.nbuf += 1
        name = "%s_%d" % (name or "sb", self.nbuf)
        t = (self.scope or self.es).enter_context(self.nc.sbuf_tensor(name, list(shape), dtype))
        return Buf(t, name)

    def psum(self, shape, dtype=F32, name=None):
        self.nbuf += 1
        name = "%s_%d" % (name or "ps", self.nbuf)
        t = (self.scope or self.es).enter_context(self.nc.psum_tensor(name, list(shape), dtype))
        return Buf(t, name)

    def barrier(self):
        for e in self.ENGS:
            eng = getattr(self.nc, e)
            seen = self.seen[e]
            for k, c in self.cnt.items():
                if c > 0 and seen.get(k, 0) < c:
                    eng.wait_ge(self.sem[k], c)
                    seen[k] = c
                    self.n_ins += 1

    def begin_phase(self):
        self.scope = ExitStack()

    def end_phase(self):
        self.barrier()
        self.scope.close()
        self.scope = None

    def dram(self, name, shape, dtype, kind="Internal"):
        t = self.nc.dram_tensor(name, list(shape), dtype, kind=kind)
        return Buf(t.ap(), name)

    def _waits(self, eng, reads, writes, extra=None):
        needs = {}
        seen = self.seen[eng]

        def need(p):
            if p is None:
                return
            k, c = p
            if k == eng and eng == "tensor":
                return
            if seen.get(k, 0) >= c:
                return
            if needs.get(k, 0) < c:
                needs[k] = c

        for b in reads:
            need(b.w)
        for b in writes:
            need(b.w)
            for k, c in b.r.items():
                need((k, c))
        if extra is not None:
            need(extra)
        e = getattr(self.nc, eng)
        for k, c in needs.items():
            e.wait_ge(self.sem[k], c)
            seen[k] = c
            self.n_ins += 1

    def op(self, eng, fn, reads=(), writes=()):
        self._waits(eng, reads, writes)
        ins = fn(getattr(self.nc, eng))
        self.cnt[eng] += 1
        c = self.cnt[eng]
        ins.then_inc(self.sem[eng], 1)
        self.n_ins += 1
        for b in reads:
            b.r[eng] = c
        for b in writes:
            b.w = (eng, c)
            b.r = {}
        return ins

    def dma(self, iss, out, in_, reads=(), writes=(), **kw):
        k = self.dnext[iss]
        self.dnext[iss] = (k + 1) % self.NDSEM
        key = "d_%s%d" % (iss, k)
        prev = (key, self.cnt[key]) if self.cnt[key] > 0 else None
        self._waits(iss, reads, writes, extra=prev)
        ins = getattr(self.nc, iss).dma_start(out=out, in_=in_, **kw)
        self.cnt[key] += 16
        c = self.cnt[key]
        ins.then_inc(self.sem[key], 16)
        self.n_ins += 1
        for b in reads:
            b.r[key] = c
        for b in writes:
            b.w = (key, c)
            b.r = {}
        return ins

    def finish(self):
        s = self.nc.sync
        for key, c in self.cnt.items():
            if (key.startswith("d_") or key == "cc") and c > 0:
                s.wait_ge(self.sem[key], c)
        for e in self.ENGS:
            if e != "sync" and self.cnt[e] > 0:
                s.wait_ge(self.sem[e], self.cnt[e])
        self.es.close()
        return self.nc


def make_identity(p, dtype=BF16):
    ident = p.sbuf([128, 128], dtype, "ident")
    p.op("gpsimd", lambda e: e.memset(ident[:], 0.0), writes=[ident])
    p.op("gpsimd", lambda e: e.affine_select(
        out=ident[:], in_=ident[:], compare_op=ALU.not_equal, fill=1.0,
        base=0, pattern=[[-1, 128]], channel_multiplier=1),
        reads=[ident], writes=[ident])
    return ident


class NormT:
    def __init__(self, p, ident, wbc, banks):
        self.p = p
        self.ident = ident
        self.wbc = wbc
        self.hbuf = [p.sbuf([128, D], F32, "hld") for _ in range(2)]
        self.sq = p.sbuf([128, D], BF16, "sqjunk")
        self.hn = [p.sbuf([128, D], BF16, "hn") for _ in range(2)]
        self.ss = [p.sbuf([128, 1], F32, "ss") for _ in range(2)]
        self.rstd = [p.sbuf([128, 1], F32, "rstd") for _ in range(2)]
        self.tp = banks
        self.i = 0

    def run(self, h_rows_ap, nrows, dst, dst_col, src_buf=None, rowscale=None):
        p = self.p
        i = self.i
        self.i += 1
        hb = self.hbuf[i % 2]
        ss, rstd, hn = self.ss[i % 2], self.rstd[i % 2], self.hn[i % 2]
        p.dma("sync", hb[0:nrows, :], h_rows_ap, reads=[src_buf] if src_buf else [], writes=[hb])
        if rowscale is not None:
            p.op("vector", lambda e: e.tensor_scalar(hb[0:nrows, :], hb[0:nrows, :], rowscale[0:nrows, 0:1], None, ALU.mult),
                 reads=[hb, rowscale], writes=[hb])
        p.op("scalar", lambda e: e.activation(out=self.sq[0:nrows, :], in_=hb[0:nrows, :],
                                               func=AF.Square, accum_out=ss[0:nrows, :]),
             reads=[hb], writes=[self.sq, ss])
        p.op("vector", lambda e: e.tensor_scalar(rstd[0:nrows, :], ss[0:nrows, :], 1.0 / D, EPS,
                                                 ALU.mult, ALU.add), reads=[ss], writes=[rstd])
        p.op("scalar", lambda e: e.sqrt(rstd[0:nrows, :], rstd[0:nrows, :]), reads=[rstd], writes=[rstd])
        p.op("vector", lambda e: e.reciprocal(rstd[0:nrows, :], rstd[0:nrows, :]), reads=[rstd], writes=[rstd])
        p.op("vector", lambda e: e.scalar_tensor_tensor(
            out=hn[0:nrows, :], in0=hb[0:nrows, :], scalar=rstd[0:nrows, :], in1=self.wbc[0:nrows, :],
            op0=ALU.mult, op1=ALU.mult), reads=[hb, rstd, self.wbc], writes=[hn])
        for half in range(2):
            tpb = self.tp[half]
            tp = tpb[:].bitcast(BF16).rearrange("p (j n) -> p j n", j=8)
            for j in range(8):
                kc = half * 8 + j
                p.op("tensor", lambda e, kc=kc, j=j: e.transpose(
                    tp[:, j, 0:nrows], hn[0:nrows, kc * 128:(kc + 1) * 128], self.ident[0:nrows, 0:nrows]),
                    reads=[hn, self.ident], writes=[tpb])
            if half == 0:
                p.op("scalar", lambda e: e.copy(
                    out=dst[:, 0:8, dst_col:dst_col + nrows], in_=tp[:, :, 0:nrows]),
                    reads=[tpb], writes=[dst])
            else:
                p.op("vector", lambda e: e.tensor_copy(
                    out=dst[:, 8:16, dst_col:dst_col + nrows], in_=tp[:, :, 0:nrows]),
                    reads=[tpb], writes=[dst])
        return hb


def alloc_banks(p):
    return [p.psum([128, 512], F32, "bank") for _ in range(8)]


def wview(w_ap, f0, fn):
    return w_ap.rearrange("(kc p) f -> p kc f", p=128)[:, :, f0:f0 + fn]


def ffn_phase(p, banks, ident, hm, halo, halo_ap, nwbc_d, wg, wu, cwT_d, wd, out, NT=512, FS=256, flag_ap=None):
    NFC = DFF // 128
    p.begin_phase()
    flag = None
    if flag_ap is not None:
        flag = p.sbuf([128, 1], F32, "flag")
        p.dma("sync", flag[:], flag_ap, writes=[flag])
    wbc = p.sbuf([128, D], F32, "wbc")
    p.dma("sync", wbc[:], nwbc_d, writes=[wbc])
    cwT = p.sbuf([128, NFC, 4], F32, "cwT")
    p.dma("sync", cwT[:], cwT_d, writes=[cwT])
    norm = NormT(p, ident, wbc, banks[6:8])
    hnT = p.sbuf([128, KC, NT + 2], BF16, "hnT")
    actT = p.sbuf([128, NFC, NT], BF16, "actT")
    nfl = FS // 128
    wgs = [p.sbuf([128, KC, FS], BF16, "wgs") for _ in range(2)]
    wus = [p.sbuf([128, KC, FS], BF16, "wus") for _ in range(2)]
    WDG = 4
    wds = [p.sbuf([128, WDG, 1024], BF16, "wds") for _ in range(2)]
    gs = [p.sbuf([128, NT + 2], F32, "gs") for _ in range(2)]
    acc = [p.sbuf([128, NT], F32, "acc") for _ in range(2)]
    sl = [p.sbuf([128, NT], F32, "sl") for _ in range(2)]
    res = [p.sbuf([128, 1024], F32, "res") for _ in range(2)]
    stg = [p.sbuf([128, 1024], F32, "stg") for _ in range(2)]
    nslab = 0
    nwd = 0
    nev = 0
    for tb in range(TPC // NT):
        t0 = tb * NT
        if tb == 0:
            norm.run(halo_ap, 2, hnT, 0, src_buf=halo, rowscale=flag)
        else:
            norm.run(hm[t0 - 2:t0, :], 2, hnT, 0, src_buf=hm)
        for tt in range(NT // 128):
            norm.run(hm[t0 + tt * 128:t0 + (tt + 1) * 128, :], 128, hnT, 2 + tt * 128, src_buf=hm)
        for fs in range(DFF // FS):
            sg, su = wgs[nslab % 2], wus[nslab % 2]
            nslab += 1
            p.dma("sync", sg[:], wview(wg.t, fs * FS, FS), reads=[wg], writes=[sg])
            p.dma("sync", su[:], wview(wu.t, fs * FS, FS), reads=[wu], writes=[su])
            for fl in range(nfl):
                fc = fs * nfl + fl
                par = fc % 2
                G, U, H = banks[par], banks[2 + par], banks[4 + par]
                for kc in range(KC):
                    p.op("tensor", lambda e, kc=kc: e.matmul(
                        G[:, 0:NT], sg[:, kc, fl * 128:(fl + 1) * 128], hnT[:, kc, 2:NT + 2],
                        start=(kc == 0), stop=(kc == KC - 1)), reads=[sg, hnT], writes=[G])
                for kc in range(KC):
                    p.op("tensor", lambda e, kc=kc: e.matmul(
                        H[:, 0:2], sg[:, kc, fl * 128:(fl + 1) * 128], hnT[:, kc, 0:2],
                        start=(kc == 0), stop=(kc == KC - 1)), reads=[sg, hnT], writes=[H])
                for kc in range(KC):
                    p.op("tensor", lambda e, kc=kc: e.matmul(
                        U[:, 0:NT], su[:, kc, fl * 128:(fl + 1) * 128], hnT[:, kc, 2:NT + 2],
                        start=(kc == 0), stop=(kc == KC - 1)), reads=[su, hnT], writes=[U])
                g, a, s = gs[par], acc[par], sl[par]
                p.op("scalar", lambda e: e.copy(out=g[:, 2:NT + 2], in_=G[:, 0:NT]), reads=[G], writes=[g])
                p.op("scalar", lambda e: e.copy(out=g[:, 0:2], in_=H[:, 0:2]), reads=[H], writes=[g])
                p.op("vector", lambda e: e.tensor_scalar(
                    a[:], g[:, 2:NT + 2], cwT[:, fc, 2:3], cwT[:, fc, 3:4], ALU.mult, ALU.add),
                    reads=[g, cwT], writes=[a])
                p.op("vector", lambda e: e.scalar_tensor_tensor(
                    out=a[:], in0=g[:, 1:NT + 1], scalar=cwT[:, fc, 1:2], in1=a[:], op0=ALU.mult, op1=ALU.add),
                    reads=[g, cwT, a], writes=[a])
                p.op("vector", lambda e: e.scalar_tensor_tensor(
                    out=a[:], in0=g[:, 0:NT], scalar=cwT[:, fc, 0:1], in1=a[:], op0=ALU.mult, op1=ALU.add),
                    reads=[g, cwT, a], writes=[a])
                p.op("scalar", lambda e: e.activation(out=s[:], in_=a[:], func=AF.Silu), reads=[a], writes=[s])
                p.op("vector", lambda e: e.tensor_tensor(actT[:, fc, :], s[:], U[:, 0:NT], ALU.mult),
                     reads=[s, U], writes=[actT])
        for dh in range(2):
            for fg in range(NFC // WDG):
                wdb = wds[nwd % 2]
                nwd += 1
                src = wd.t.rearrange("(fc p) d -> p fc d", p=128)[:, fg * WDG:(fg + 1) * WDG, dh * 1024:(dh + 1) * 1024]
                p.dma("sync", wdb[:], src, reads=[wd], writes=[wdb])
                for fl in range(WDG):
                    fc = fg * WDG + fl
                    for tt in range(NT // 128):
                        for dq in range(2):
                            bk = banks[tt * 2 + dq]
                            p.op("tensor", lambda e, tt=tt, dq=dq, bk=bk: e.matmul(
                                bk[:, :], actT[:, fc, tt * 128:(tt + 1) * 128], wdb[:, fl, dq * 512:(dq + 1) * 512],
                                start=(fc == 0), stop=(fc == NFC - 1)), reads=[actT, wdb], writes=[bk])
            for tt in range(NT // 128):
                r, st = res[nev % 2], stg[nev % 2]
                nev += 1
                rows = slice(t0 + tt * 128, t0 + (tt + 1) * 128)
                p.dma("sync", r[:], hm[rows, dh * 1024:(dh + 1) * 1024], reads=[hm], writes=[r])
                for dq in range(2):
                    p.op("vector", lambda e, dq=dq: e.tensor_tensor(
                        st[:, dq * 512:(dq + 1) * 512], banks[tt * 2 + dq][:, :], r[:, dq * 512:(dq + 1) * 512], ALU.add),
                        reads=[banks[tt * 2 + dq], r], writes=[st])
                p.dma("sync", out[rows, dh * 1024:(dh + 1) * 1024], st[:], reads=[st], writes=[out])
    p.end_phase()


def evac(p, i, out_ap, in_ap, reads, writes):
    if i % 2 == 0:
        p.op("scalar", lambda e: e.copy(out=out_ap, in_=in_ap), reads=reads, writes=writes)
    else:
        p.op("vector", lambda e: e.tensor_copy(out=out_ap, in_=in_ap), reads=reads, writes=writes)


def proj_phase(p, banks, ident, h, nwbc_d, W, specs):
    p.begin_phase()
    wbc = p.sbuf([128, D], F32, "wbc")
    p.dma("sync", wbc[:], nwbc_d, writes=[wbc])
    norm = NormT(p, ident, wbc, banks[6:8])
    hnT = [p.sbuf([128, KC, 512], BF16, "hnT") for _ in range(TPC // 512)]
    for tt in range(TPC // 128):
        norm.run(h[tt * 128:(tt + 1) * 128, :], 128, hnT[tt // 4], (tt % 4) * 128, src_buf=h)
    WS = 512
    wsl = [p.sbuf([128, KC, WS], BF16, "wsl") for _ in range(2)]
    stT = [p.sbuf([128, TPC], F32, "stT") for _ in range(2)]
    stN = [p.sbuf([128, 512], F32, "stN") for _ in range(3)]
    nsl = 0
    nb = 0
    nst = 0
    for (c0, ncols, orient, out, dt) in specs:
        for s0 in range(0, ncols, WS):
            sw = min(WS, ncols - s0)
            ws = wsl[nsl % 2]
            nsl += 1
            p.dma("sync", ws[:, :, 0:sw], wview(W.t, c0 + s0, sw), reads=[W], writes=[ws])
            if orient == 'T':
                for fl in range(sw // 128):
                    st = stT[nst % 2]
                    nst += 1
                    stv = st[:].bitcast(BF16)[:, 0:TPC] if dt == BF16 else st[:, 0:TPC]
                    for tb in range(TPC // 512):
                        bk = banks[nb % 6]
                        nb += 1
                        for kc in range(KC):
                            p.op("tensor", lambda e, kc=kc: e.matmul(
                                bk[:, :], ws[:, kc, fl * 128:(fl + 1) * 128], hnT[tb][:, kc, :],
                                start=(kc == 0), stop=(kc == KC - 1)), reads=[ws, hnT[tb]], writes=[bk])
                        evac(p, nb, stv[:, tb * 512:(tb + 1) * 512], bk[:, :], [bk], [st])
                    r0 = s0 + fl * 128
                    p.dma("sync", out[r0:r0 + 128, :], stv, reads=[st], writes=[out])
            else:
                for tt in range(TPC // 128):
                    bk = banks[nb % 6]
                    nb += 1
                    for kc in range(KC):
                        p.op("tensor", lambda e, kc=kc: e.matmul(
                            bk[:, 0:sw], hnT[tt // 4][:, kc, (tt % 4) * 128:(tt % 4 + 1) * 128], ws[:, kc, 0:sw],
                            start=(kc == 0), stop=(kc == KC - 1)), reads=[ws, hnT[tt // 4]], writes=[bk])
                    st = stN[nst % 3]
                    nst += 1
                    stv = st[:].bitcast(BF16)[:, 0:sw] if dt == BF16 else st[:, 0:sw]
                    evac(p, nb, stv, bk[:, 0:sw], [bk], [st])
                    p.dma("sync", out[tt * 128:(tt + 1) * 128, s0:s0 + sw], stv, reads=[st], writes=[out])
    p.end_phase()


def weight_prep(p, shard, full, rows, cols):
    sb = p.dram(shard.name + "_b", [rows, cols], BF16)
    P_ = max(d for d in range(1, 129) if rows % d == 0)
    J = rows // P_
    CH = max(1, min(J, 16384 // (cols * 2) if cols * 2 <= 16384 else 1))
    p.begin_phase()
    tl = [p.sbuf([128, 8192], BF16, "wcast") for _ in range(2)]
    srcv = shard.t.rearrange("(p j) c -> p j c", p=P_)
    dstv = sb.t.rearrange("(p j) c -> p j c", p=P_)
    i = 0
    per = max(1, 8192 // cols)
    for j0 in range(0, J, per):
        jn = min(per, J - j0)
        t = tl[i % 2]
        i += 1
        tv = t[0:P_, 0:jn * cols].rearrange("p (j c) -> p j c", c=cols)
        p.dma("gpsimd", tv, srcv[:, j0:j0 + jn, :], reads=[shard], writes=[t])
        p.dma("sync", dstv[:, j0:j0 + jn, :], tv, reads=[t], writes=[sb])
    p.end_phase()
    p.allgather(sb, full)


def make_masks_sb(p):
    m32, m16 = [], []
    for r in range(4):
        a = p.sbuf([128, 512], F32, "msk32")
        p.op("gpsimd", lambda e: e.memset(a[:], 1.0), writes=[a])
        p.op("gpsimd", lambda e, r=r: e.affine_select(
            out=a[:], in_=a[:], compare_op=ALU.is_gt, fill=0.0, base=-r * 128,
            pattern=[[1, 512]], channel_multiplier=-1), reads=[a], writes=[a])
        b = p.sbuf([128, 512], BF16, "msk16")
        p.op("gpsimd", lambda e: e.tensor_copy(out=b[:], in_=a[:]), reads=[a], writes=[b])
        m32.append(a)
        m16.append(b)
    nt32 = p.sbuf([128, 128], F32, "nt32")
    p.op("gpsimd", lambda e: e.memset(nt32[:], -1.0), writes=[nt32])
    p.op("gpsimd", lambda e: e.affine_select(
        out=nt32[:], in_=nt32[:], compare_op=ALU.is_gt, fill=0.0, base=0,
        pattern=[[-1, 128]], channel_multiplier=1), reads=[nt32], writes=[nt32])
    nt = p.sbuf([128, 128], BF16, "nt")
    p.op("gpsimd", lambda e: e.tensor_copy(out=nt[:], in_=nt32[:]), reads=[nt32], writes=[nt])
    nr = p.sbuf([128, 128], BF16, "nr")
    p.op("gpsimd", lambda e: e.tensor_scalar(nr[:], nt32[:], -1.0, -1.0, ALU.mult, ALU.add),
         reads=[nt32], writes=[nr])
    return m32, m16, nt, nr


def static_srcs(qT, kT, v, T):
    nblk = max(1, T // 2048)
    bl = T // nblk
    qf = lambda u, r: qT[u * 128:(u + 1) * 128, r * bl:(r + 1) * bl]
    kf = lambda u, r: kT[u * 128:(u + 1) * 128, r * bl:(r + 1) * bl]
    vf = lambda u, r: v[r * bl:(r + 1) * bl, u * 128:(u + 1) * 128]
    return qf, kf, vf, nblk


def sb_attn_phase(p, banks, qT, kT, v, oT, NU=2, T=SEQ, srcs=None):
    p.begin_phase()
    qf, kf, vf, nblk = srcs or static_srcs(qT, kT, v, T)
    bl = T // nblk
    scale = 128.0 ** -0.5
    m32, m16, nt, nr = make_masks_sb(p)
    NQ = T // 512
    NKB = T // 128
    qs, ks, vs = [], [], []
    for u in range(NU):
        q = p.sbuf([128, T], BF16, "qs")
        k = p.sbuf([128, T], BF16, "ks")
        vv = p.sbuf([128, NKB, 128], BF16, "vs")
        for r in range(nblk):
            p.dma("sync", q[:, r * bl:(r + 1) * bl], qf(u, r), reads=[qT], writes=[q])
            p.dma("sync", k[:, r * bl:(r + 1) * bl], kf(u, r), reads=[kT], writes=[k])
            p.dma("sync", vv[:, r * (bl // 128):(r + 1) * (bl // 128), :],
                  vf(u, r).rearrange("(kb p) f -> p kb f", p=128), reads=[v], writes=[vv])
        qs.append(q); ks.append(k); vs.append(vv)
    NB = 6
    e_t = [p.sbuf([128, 512], F32, "e_t") for _ in range(NB)]
    sp_t = [p.sbuf([128, 512], F32, "sp_t") for _ in range(NB)]
    spb_t = [p.sbuf([128, 512], BF16, "spb_t") for _ in range(NB)]
    lb_t = [p.sbuf([128, 512], F32, "lb_t") for _ in range(NB)]
    arg_t = e_t
    w_t = [p.sbuf([128, 512], BF16, "w_t") for _ in range(NB)]
    ost = [p.sbuf([128, 512], BF16, "ost") for _ in range(2)]
    Sb = [banks[0], banks[1], banks[2], banks[7]]
    Cb = [banks[3], banks[4]]
    Ob = [banks[5], banks[6]]
    tiles = []
    for qi in range(NQ):
        nkb = 4 * qi + 4
        for jj in range(nkb):
            kb = nkb - 1 - jj
            for u in range(NU):
                tiles.append((qi, kb, u, jj == 0, kb == 0))
    n = len(tiles)
    nost = [0]

    def ops_S(i):
        qi, kb, u, first, last = tiles[i]
        S = Sb[i % 4]
        return [lambda: p.op("tensor", lambda e: e.matmul(
            S[:, :], ks[u][:, kb * 128:(kb + 1) * 128], qs[u][:, qi * 512:(qi + 1) * 512],
            start=True, stop=True), reads=[ks[u], qs[u]], writes=[S])]

    def ops_mid(i):
        qi, kb, u, first, last = tiles[i]
        S = Sb[i % 4]
        C = Cb[u]
        b = i % NB
        r = kb - 4 * qi
        o = []
        o.append(lambda: p.op("scalar", lambda e: e.activation(out=e_t[b][:], in_=S[:, :], func=AF.Exp, scale=scale),
                              reads=[S], writes=[e_t[b]]))
        o.append(lambda: p.op("scalar", lambda e: e.activation(out=sp_t[b][:], in_=e_t[b][:], func=AF.Ln, bias=1.0, scale=1.0),
                              reads=[e_t[b]], writes=[sp_t[b]]))
        if r >= 0:
            o.append(lambda: p.op("gpsimd", lambda e: e.tensor_tensor(spb_t[b][:], sp_t[b][:], m32[r][:], ALU.mult),
                                  reads=[sp_t[b], m32[r]], writes=[spb_t[b]]))
        else:
            o.append(lambda: p.op("gpsimd", lambda e: e.tensor_copy(out=spb_t[b][:], in_=sp_t[b][:]),
                                  reads=[sp_t[b]], writes=[spb_t[b]]))
        o.append(lambda: p.op("vector", lambda e: e.scalar_tensor_tensor(
            out=lb_t[b][:], in0=S[:, :], scalar=scale, in1=sp_t[b][:], op0=ALU.mult, op1=ALU.subtract),
            reads=[S, sp_t[b]], writes=[lb_t[b]]))
        o.append(lambda: p.op("tensor", lambda e: e.matmul(C[:, :], nt[:], spb_t[b][:], start=first, stop=False),
                              reads=[nt, spb_t[b]], writes=[C]))
        o.append(lambda: p.op("vector", lambda e: e.tensor_tensor(arg_t[b][:], C[:, :], lb_t[b][:], ALU.add),
                              reads=[C, lb_t[b]], writes=[arg_t[b]]))
        o.append(lambda: p.op("tensor", lambda e: e.matmul(C[:, :], nr[:], spb_t[b][:], start=False, stop=last),
                              reads=[nr, spb_t[b]], writes=[C]))
        return o

    def ops_end(i):
        qi, kb, u, first, last = tiles[i]
        O = Ob[u]
        b = i % NB
        r = kb - 4 * qi
        o = []
        o.append(lambda: p.op("scalar", lambda e: e.activation(out=w_t[b][:], in_=arg_t[b][:], func=AF.Exp),
                              reads=[arg_t[b]], writes=[w_t[b]]))
        if r >= 0:
            o.append(lambda: p.op("gpsimd", lambda e: e.tensor_tensor(w_t[b][:], w_t[b][:], m16[r][:], ALU.mult),
                                  reads=[w_t[b], m16[r]], writes=[w_t[b]]))
        o.append(lambda: p.op("tensor", lambda e: e.matmul(O[:, :], vs[u][:, kb, :], w_t[b][:], start=first, stop=last),
                              reads=[vs[u], w_t[b]], writes=[O]))
        if last:
            def fin():
                st = ost[nost[0] % 2]
                nost[0] += 1
                p.op("vector", lambda e: e.tensor_copy(out=st[:], in_=O[:, :]), reads=[O], writes=[st])
                p.dma("sync", oT[u * 128:(u + 1) * 128, qi * 512:(qi + 1) * 512], st[:], reads=[st], writes=[oT])
            o.append(fin)
        return o

    def interleave(lists):
        m = max(len(l) for l in lists)
        for k in range(m):
            for l in lists:
                if k < len(l):
                    l[k]()

    npair = (n + NU - 1) // NU
    for s in range(npair + 2):
        if s < npair:
            interleave([ops_S(i) for i in range(s * NU, min(n, (s + 1) * NU))])
        if 1 <= s <= npair:
            interleave([ops_mid(i) for i in range((s - 1) * NU, min(n, s * NU))])
        if s >= 2:
            interleave([ops_end(i) for i in range((s - 2) * NU, min(n, (s - 1) * NU))])
    p.end_phase()


def tri_const(p, keep_op, val=1.0, dtype=F32, name="tri", cm=1, step=-1):
    a = p.sbuf([128, 128], F32, name)
    p.op("gpsimd", lambda e: e.memset(a[:], val), writes=[a])
    p.op("gpsimd", lambda e: e.affine_select(
        out=a[:], in_=a[:], compare_op=keep_op, fill=0.0, base=0,
        pattern=[[step, 128]], channel_multiplier=cm), reads=[a], writes=[a])
    if dtype == F32:
        return a
    b = p.sbuf([128, 128], dtype, name + "b")
    p.op("gpsimd", lambda e: e.tensor_copy(out=b[:], in_=a[:]), reads=[a], writes=[b])
    return b


def bc_last(ap2, n):
    return ap2.unsqueeze(2).broadcast_to([ap2.shape[0], ap2.shape[1], n])


def ssd_phase(p, banks, xT, dtl, cw4, dtb, alog, dsk, y, T=SEQ, xsrc=None, dt_parts=None, xdeps=None):
    p.begin_phase()
    NCH = T // 128
    SEGL = 2048 if T >= 2048 else T
    NSEG = T // SEGL
    ident32 = make_identity(p, F32)
    identb = make_identity(p, BF16)
    TRIU = tri_const(p, ALU.is_ge, name="triu", cm=-1, step=1)
    T1 = tri_const(p, ALU.is_gt, name="t1")
    ONES = p.sbuf([128, 128], F32, "ones")
    p.op("gpsimd", lambda e: e.memset(ONES[:], 1.0), writes=[ONES])
    cw = p.sbuf([128, 4, 5], F32, "cw")
    if isinstance(cw4, (list, tuple)):
        for i_, ap_ in enumerate(cw4):
            p.dma("sync", cw[:, i_, :], ap_, writes=[cw])
    else:
        p.dma("sync", cw[:], cw4, writes=[cw])
    dsk_t = p.sbuf([128, 256], F32, "dsk")
    p.dma("sync", dsk_t[:], dsk, writes=[dsk_t])
    dt = p.sbuf([128, NCH, 4], F32, "dt")
    if dt_parts is None:
        p.dma("sync", dt[:], dtl, writes=[dt])
    else:
        npart = len(dt_parts)
        for i_, (ap_, dep_) in enumerate(dt_parts):
            p.dma("sync", dt[:, i_ * (NCH // npart):(i_ + 1) * (NCH // npart), :], ap_, reads=[dep_], writes=[dt])
    if xsrc is None:
        xdeps = [xT]
        xsrc = lambda ti, sg, halo: (xT[ti * 128:(ti + 1) * 128, sg * SEGL - 3:sg * SEGL] if halo
                                     else xT[ti * 128:(ti + 1) * 128, sg * SEGL:(sg + 1) * SEGL])
    dtb_t = p.sbuf([128, 4], F32, "dtb")
    p.dma("sync", dtb_t[:], dtb, writes=[dtb_t])
    a_t = p.sbuf([128, 4], F32, "a_t")
    p.dma("sync", a_t[:], alog, writes=[a_t])
    p.op("scalar", lambda e: e.activation(out=a_t[:], in_=a_t[:], func=AF.Exp), reads=[a_t], writes=[a_t])
    p.op("vector", lambda e: e.tensor_scalar(a_t[:], a_t[:], -1.0, None, ALU.mult), reads=[a_t], writes=[a_t])
    bcast_c = lambda t2: t2[:].unsqueeze(1).broadcast_to([128, NCH, 4])
    p.op("vector", lambda e: e.tensor_tensor(dt[:], dt[:], bcast_c(dtb_t), ALU.add), reads=[dt, dtb_t], writes=[dt])
    p.op("scalar", lambda e: e.activation(out=dt[:], in_=dt[:], func=AF.Exp), reads=[dt], writes=[dt])
    p.op("scalar", lambda e: e.activation(out=dt[:], in_=dt[:], func=AF.Ln, bias=1.0, scale=1.0), reads=[dt], writes=[dt])
    dta = p.sbuf([128, NCH, 4], F32, "dta")
    p.op("vector", lambda e: e.tensor_tensor(dta[:], dt[:], bcast_c(a_t), ALU.mult), reads=[dt, a_t], writes=[dta])
    ea = p.sbuf([128, NCH, 4], F32, "ea")
    dte = p.sbuf([128, NCH, 4], F32, "dte")
    cd = p.sbuf([128, NCH, 4], F32, "cd")
    dtaf = dta[:].rearrange("p c h -> p (c h)")
    for (mat, dst, bk) in ((TRIU, ea, banks[0]), (T1, dte, banks[1]), (ONES, cd, banks[2])):
        w = NCH * 4
        p.op("tensor", lambda e, mat=mat, bk=bk: e.matmul(bk[:, 0:w], mat[:], dtaf, start=True, stop=True),
             reads=[mat, dta], writes=[bk])
        p.op("scalar", lambda e, dst=dst, bk=bk: e.activation(
            out=dst[:].rearrange("p c h -> p (c h)"), in_=bk[:, 0:w], func=AF.Exp), reads=[bk], writes=[dst])
    dtdte = p.sbuf([128, NCH, 4], F32, "dtdte")
    p.op("vector", lambda e: e.tensor_tensor(dtdte[:], dt[:], dte[:], ALU.mult), reads=[dt, dte], writes=[dtdte])
    xtok = p.sbuf([128, NCH, 256], F32, "xtok")
    Btok = p.sbuf([128, NCH, 128], BF16, "Btok")
    BT = p.sbuf([128, T], BF16, "BT")
    CT = p.sbuf([128, T], BF16, "CT")
    segb = [p.sbuf([128, 3 + SEGL], F32, "segb") for _ in range(2)]
    xc = [p.sbuf([128, SEGL], F32, "xc") for _ in range(2)]
    xcb = p.sbuf([128, SEGL], BF16, "xcb")
    ns = 0
    ntp = 0
    for ti in range(4):
        for sg in range(NSEG):
            sbf = segb[ns % 2]
            xo = xc[ns % 2]
            ns += 1
            c0 = sg * SEGL
            if sg == 0:
                p.op("gpsimd", lambda e: e.memset(sbf[:, 0:3], 0.0), writes=[sbf])
            else:
                p.dma("sync", sbf[:, 0:3], xsrc(ti, sg, True), reads=xdeps, writes=[sbf])
            p.dma("sync", sbf[:, 3:3 + SEGL], xsrc(ti, sg, False), reads=xdeps, writes=[sbf])
            p.op("vector", lambda e: e.tensor_scalar(xo[:], sbf[:, 3:3 + SEGL], cw[:, ti, 3:4], cw[:, ti, 4:5],
                                                     ALU.mult, ALU.add), reads=[sbf, cw], writes=[xo])
            for k in range(3):
                p.op("vector", lambda e, k=k: e.scalar_tensor_tensor(
                    out=xo[:], in0=sbf[:, k:k + SEGL], scalar=cw[:, ti, k:k + 1], in1=xo[:],
                    op0=ALU.mult, op1=ALU.add), reads=[sbf, cw, xo], writes=[xo])
            if ti < 2:
                p.op("scalar", lambda e: e.activation(out=xo[:], in_=xo[:], func=AF.Silu), reads=[xo], writes=[xo])
                for cc in range(SEGL // 128):
                    ch = c0 // 128 + cc
                    bk = banks[6 + ntp % 2]
                    ntp += 1
                    p.op("tensor", lambda e, cc=cc, bk=bk: e.transpose(
                        bk[:, 0:128], xo[:, cc * 128:(cc + 1) * 128], ident32[:]), reads=[xo, ident32], writes=[bk])
                    evac(p, ntp, xtok[:, ch, ti * 128:(ti + 1) * 128], bk[:, 0:128], [bk], [xtok])
            elif ti == 2:
                p.op("scalar", lambda e: e.activation(out=BT[:, c0:c0 + SEGL], in_=xo[:], func=AF.Silu),
                     reads=[xo], writes=[BT])
                for cc in range(SEGL // 128):
                    ch = c0 // 128 + cc
                    bk = banks[6 + ntp % 2]
                    ntp += 1
                    bkb = bk[:].bitcast(BF16)
                    p.op("tensor", lambda e, cc=cc, bkb=bkb: e.transpose(
                        bkb[:, 0:128], BT[:, c0 + cc * 128:c0 + (cc + 1) * 128], identb[:]),
                        reads=[BT, identb], writes=[bk])
                    evac(p, ntp, Btok[:, ch, :], bkb[:, 0:128], [bk], [Btok])
            else:
                p.op("scalar", lambda e: e.activation(out=CT[:, c0:c0 + SEGL], in_=xo[:], func=AF.Silu),
                     reads=[xo], writes=[CT])
    H = p.sbuf([128, 256], F32, "H")
    Hb = p.sbuf([128, 256], BF16, "Hb")
    p.op("vector", lambda e: e.memset(H[:], 0.0), writes=[H])
    p.op("vector", lambda e: e.memset(Hb[:], 0.0), writes=[Hb])
    R = [p.sbuf([128, 4, 128], F32, "R") for _ in range(2)]
    E = [p.sbuf([128, 4, 128], F32, "E") for _ in range(2)]
    SM = [p.sbuf([128, 128], F32, "SM") for _ in range(2)]
    M = [p.sbuf([128, 4, 128], BF16, "M") for _ in range(2)]
    xdt = [p.sbuf([128, 256], BF16, "xdt") for _ in range(2)]
    xdte = [p.sbuf([128, 256], BF16, "xdte") for _ in range(2)]
    t1 = [p.sbuf([128, 256], F32, "t1") for _ in range(2)]
    yo = [p.sbuf([128, 256], F32, "yo") for _ in range(2)]
    for c in range(NCH):
        b = c % 2
        SEG, SC, YB, YO, ST = banks[0 + b], banks[2], banks[3], banks[4], banks[5]
        cs = slice(c * 128, (c + 1) * 128)
        for h in range(4):
            p.op("gpsimd", lambda e, h=h: e.tensor_scalar(R[b][:, h, :], TRIU[:], dta[:, c, h:h + 1], None, ALU.mult),
                 reads=[TRIU, dta], writes=[R[b]])
        p.op("tensor", lambda e: e.matmul(SEG[:, :], T1[:], R[b][:].rearrange("p h l -> p (h l)"), start=True, stop=True),
             reads=[T1, R[b]], writes=[SEG])
        p.op("scalar", lambda e: e.activation(out=E[b][:].rearrange("p h l -> p (h l)"), in_=SEG[:, :], func=AF.Exp),
             reads=[SEG], writes=[E[b]])
        p.op("tensor", lambda e: e.matmul(SC[:, 0:128], BT[:, cs], CT[:, cs], start=True, stop=True),
             reads=[BT, CT], writes=[SC])
        p.op("vector", lambda e: e.tensor_tensor(SM[b][:], SC[:, 0:128], TRIU[:], ALU.mult),
             reads=[SC, TRIU], writes=[SM[b]])
        p.op("vector", lambda e: e.tensor_tensor(
            M[b][:], E[b][:], SM[b][:].unsqueeze(1).broadcast_to([128, 4, 128]), ALU.mult),
            reads=[E[b], SM[b]], writes=[M[b]])
        for h in range(4):
            hs = slice(h * 64, (h + 1) * 64)
            p.op("scalar", lambda e, h=h, hs=hs: e.activation(
                out=xdt[b][:, hs], in_=xtok[:, c, hs], func=AF.Copy, scale=dt[:, c, h:h + 1]),
                reads=[xtok, dt], writes=[xdt[b]])
            p.op("scalar", lambda e, h=h, hs=hs: e.activation(
                out=xdte[b][:, hs], in_=xtok[:, c, hs], func=AF.Copy, scale=dtdte[:, c, h:h + 1]),
                reads=[xtok, dtdte], writes=[xdte[b]])
        for h in range(4):
            hs = slice(h * 64, (h + 1) * 64)
            p.op("tensor", lambda e, h=h, hs=hs: e.matmul(YB[:, hs], M[b][:, h, :], xdt[b][:, hs], start=True, stop=True),
                 reads=[M[b], xdt[b]], writes=[YB])
        p.op("tensor", lambda e: e.matmul(YO[:, 0:256], CT[:, cs], Hb[:], start=True, stop=True),
             reads=[CT, Hb], writes=[YO])
        p.op("tensor", lambda e: e.matmul(ST[:, 0:256], Btok[:, c, :], xdte[b][:], start=True, stop=True),
             reads=[Btok, xdte[b]], writes=[ST])
        v3 = lambda ap: ap.rearrange("p (h q) -> p h q", h=4)
        p.op("vector", lambda e: e.tensor_tensor(v3(t1[b][:]), v3(YO[:, 0:256]), bc_last(ea[:, c, :], 64), ALU.mult),
             reads=[YO, ea], writes=[t1[b]])
        p.op("vector", lambda e: e.tensor_tensor(t1[b][:], t1[b][:], YB[:, 0:256], ALU.add),
             reads=[t1[b], YB], writes=[t1[b]])
        p.op("gpsimd", lambda e: e.tensor_tensor(yo[b][:], xtok[:, c, :], dsk_t[:], ALU.mult),
             reads=[xtok, dsk_t], writes=[yo[b]])
        p.op("vector", lambda e: e.tensor_tensor(yo[b][:], yo[b][:], t1[b][:], ALU.add),
             reads=[yo[b], t1[b]], writes=[yo[b]])
        p.dma("sync", y[cs, :], yo[b][:], reads=[yo[b]], writes=[y])
        p.op("vector", lambda e: e.tensor_tensor(v3(H[:]), v3(H[:]), bc_last(cd[:, c, :], 64), ALU.mult),
             reads=[H, cd], writes=[H])
        p.op("vector", lambda e: e.tensor_tensor(H[:], H[:], ST[:, 0:256], ALU.add), reads=[H, ST], writes=[H])
        p.op("scalar", lambda e: e.copy(out=Hb[:], in_=H[:]), reads=[H], writes=[Hb])
    p.end_phase()


DIL = (1, 4, 16)


def dilated_phase(p, banks, qT, kT, v, oT, NU=4, T=SEQ, srcs=None):
    p.begin_phase()
    qf, kf, vf, nblk = srcs or static_srcs(qT, kT, v, T)
    bl = T // nblk
    scale = 128.0 ** -0.5
    m32 = p.sbuf([128, 256], F32, "dm32")
    p.op("gpsimd", lambda e: e.memset(m32[:], 1.0), writes=[m32])
    p.op("gpsimd", lambda e: e.affine_select(out=m32[:, 0:128], in_=m32[:, 0:128], compare_op=ALU.is_ge, fill=0.0,
                                             base=0, pattern=[[1, 128]], channel_multiplier=-1),
         reads=[m32], writes=[m32])
    p.op("gpsimd", lambda e: e.affine_select(out=m32[:, 128:256], in_=m32[:, 128:256], compare_op=ALU.is_ge, fill=0.0,
                                             base=0, pattern=[[-1, 128]], channel_multiplier=1),
         reads=[m32], writes=[m32])
    mk = p.sbuf([128, 256], BF16, "dmk")
    p.op("gpsimd", lambda e: e.tensor_copy(out=mk[:], in_=m32[:]), reads=[m32], writes=[mk])
    ones = p.sbuf([128, 128], BF16, "onesb")
    p.op("gpsimd", lambda e: e.memset(ones[:], 1.0), writes=[ones])
    qs = p.sbuf([128, T], BF16, "dqs")
    ks = p.sbuf([128, T], BF16, "dks")
    Vr = [p.sbuf([128, T // 128, 128], BF16, "dVr") for _ in range(2)]
    ACCN = p.sbuf([128, T], F32, "accn")
    ACCD = p.sbuf([128, T], F32, "accd")
    Pb = [p.sbuf([128, 256], BF16, "dP") for _ in range(4)]
    obuf = p.sbuf([128, T], BF16, "dob")
    Sb = [banks[0], banks[1]]
    Nb = [banks[2], banks[3]]
    Db = [banks[4], banks[5]]
    nS = 0
    nP = 0
    nG = 0
    nV = 0
    for u in range(NU):
        for rr in range(nblk):
            p.dma("sync", qs[:, rr * bl:(rr + 1) * bl], qf(u, rr), reads=[qT], writes=[qs])
            p.dma("sync", ks[:, rr * bl:(rr + 1) * bl], kf(u, rr), reads=[kT], writes=[ks])
        for bi, r in enumerate(DIL):
            nb = T // (128 * r)
            vr = Vr[nV % 2]
            nV += 1
            nbb = nb // nblk if nb >= nblk else 1
            for rho in range(r):
                for rr in range(nblk if nb >= nblk else 1):
                    sv = vf(u, rr) if nb >= nblk else v[:, u * 128:(u + 1) * 128]
                    sv = sv.rearrange("(m r) f -> r m f", r=r)[rho].rearrange("(kb q) f -> q kb f", q=128)
                    p.dma("sync", vr[:, rho * nb + rr * nbb:rho * nb + (rr + 1) * nbb, :], sv, reads=[v], writes=[vr])
            qv = qs[:].rearrange("p (m r) -> p r m", r=r)
            kv = ks[:].rearrange("p (m r) -> p r m", r=r)
            an = ACCN[:].rearrange("p (m r) -> p r m", r=r)
            ad = ACCD[:].rearrange("p (m r) -> p r m", r=r)
            blocks = [(rho, kb) for rho in range(r) for kb in range(nb)]
            Pof = {}

            def front(n_, r=r, nb=nb, qv=qv, kv=kv):
                nonlocal nS, nP
                rho, kb = blocks[n_]
                nq = 256 if kb < nb - 1 else 128
                S = Sb[nS % 2]
                nS += 1
                P = Pb[nP % 4]
                nP += 1
                Pof[n_] = P
                p.op("tensor", lambda e: e.matmul(S[:, 0:nq], kv[:, rho, kb * 128:(kb + 1) * 128],
                                                  qv[:, rho, kb * 128:kb * 128 + nq], start=True, stop=True),
                     reads=[ks, qs], writes=[S])
                p.op("scalar", lambda e: e.activation(out=P[:, 0:nq], in_=S[:, 0:nq], func=AF.Exp, scale=scale),
                     reads=[S], writes=[P])
                p.op("gpsimd", lambda e: e.tensor_tensor(P[:, 0:nq], P[:, 0:nq], mk[:, 0:nq], ALU.mult),
                     reads=[P, mk], writes=[P])

            def back(n_, r=r, nb=nb, vr=vr, an=an, ad=ad, bi=bi):
                nonlocal nG
                rho, kb = blocks[n_]
                P = Pof[n_]
                prevP = Pof[n_ - 1] if kb > 0 else None
                g, j = kb // 4, kb % 4
                NB_, DB_ = Nb[nG % 2], Db[nG % 2]
                cs = slice(j * 128, (j + 1) * 128)
                if prevP is not None:
                    p.op("tensor", lambda e: e.matmul(NB_[:, cs], vr[:, rho * nb + kb - 1, :], prevP[:, 128:256],
                                                      start=True, stop=False), reads=[vr, prevP], writes=[NB_])
                    p.op("tensor", lambda e: e.matmul(DB_[:, cs], ones[:], prevP[:, 128:256],
                                                      start=True, stop=False), reads=[ones, prevP], writes=[DB_])
                p.op("tensor", lambda e: e.matmul(NB_[:, cs], vr[:, rho * nb + kb, :], P[:, 0:128],
                                                  start=(prevP is None), stop=True), reads=[vr, P], writes=[NB_])
                p.op("tensor", lambda e: e.matmul(DB_[:, cs], ones[:], P[:, 0:128],
                                                  start=(prevP is None), stop=True), reads=[ones, P], writes=[DB_])
                if j == 3 or kb == nb - 1:
                    gw = (j + 1) * 128
                    ts_ = slice(g * 512, g * 512 + gw)
                    if bi == 0:
                        p.op("vector", lambda e: e.tensor_copy(out=an[:, rho, ts_], in_=NB_[:, 0:gw]),
                             reads=[NB_], writes=[ACCN])
                        p.op("vector", lambda e: e.tensor_copy(out=ad[:, rho, ts_], in_=DB_[:, 0:gw]),
                             reads=[DB_], writes=[ACCD])
                    else:
                        p.op("vector", lambda e: e.tensor_tensor(an[:, rho, ts_], NB_[:, 0:gw], an[:, rho, ts_], ALU.add),
                             reads=[NB_, ACCN], writes=[ACCN])
                        p.op("vector", lambda e: e.tensor_tensor(ad[:, rho, ts_], DB_[:, 0:gw], ad[:, rho, ts_], ALU.add),
                             reads=[DB_, ACCD], writes=[ACCD])
                    nG += 1

            front(0)
            for n_ in range(len(blocks)):
                if n_ + 1 < len(blocks):
                    front(n_ + 1)
                back(n_)
        for hh in range(T // 2048):
            hs = slice(hh * 2048, (hh + 1) * 2048)
            p.op("vector", lambda e: e.reciprocal(ACCD[:, hs], ACCD[:, hs]), reads=[ACCD], writes=[ACCD])
            p.op("vector", lambda e: e.tensor_tensor(obuf[:, hs], ACCN[:, hs], ACCD[:, hs], ALU.mult),
                 reads=[ACCN, ACCD], writes=[obuf])
        p.dma("sync", oT[u * 128:(u + 1) * 128, :], obuf[:], reads=[obuf], writes=[oT])
    p.end_phase()


def outproj_phase(p, banks, ident, h, o_src, n_att, Wout, out, gate=None):
    p.begin_phase()
    Wsb = p.sbuf([128, KC, D], BF16, "Wout")
    for q4 in range(4):
        p.dma("sync", Wsb[:, q4 * 4:(q4 + 1) * 4, :], Wout.t.rearrange("(kc p) f -> p kc f", p=128)[:, q4 * 4:(q4 + 1) * 4, :],
              reads=[Wout], writes=[Wsb])
    oTt = [p.sbuf([128, KC, 128], BF16, "oTt") for _ in range(2)]
    hres = [p.sbuf([128, D], F32, "hres") for _ in range(2)]
    stg = [p.sbuf([128, D], F32, "ostg") for _ in range(2)]
    if gate is not None:
        y_src, zb, gnw_ap, gdeps = gate
        gnw = p.sbuf([128, 1024], F32, "gnw")
        p.dma("sync", gnw[:], gnw_ap, writes=[gnw])
        yt = [p.sbuf([128, 1024], F32, "yt") for _ in range(2)]
        zt = [p.sbuf([128, 1024], F32, "zt") for _ in range(2)]
        gb = [p.sbuf([128, 1024], BF16, "gb") for _ in range(2)]
        junk = p.sbuf([128, 1024], BF16, "gjunk")
        gss = [p.sbuf([128, 1], F32, "gss") for _ in range(2)]
    att_deps = o_src.deps
    for tt in range(TPC // 128):
        b = tt % 2
        ot = oTt[b]
        for kc in range(n_att):
            p.dma("sync", ot[:, kc, :], o_src(kc, tt), reads=att_deps, writes=[ot])
        if gate is not None:
            for hq in range(4):
                p.dma("sync", yt[b][:, hq * 256:(hq + 1) * 256], y_src(hq, tt), reads=gdeps, writes=[yt[b]])
            p.dma("sync", zt[b][:], zb[tt * 128:(tt + 1) * 128, :], reads=[zb], writes=[zt[b]])
            p.op("scalar", lambda e: e.activation(out=zt[b][:], in_=zt[b][:], func=AF.Silu), reads=[zt[b]], writes=[zt[b]])
            p.op("vector", lambda e: e.tensor_tensor(yt[b][:], yt[b][:], zt[b][:], ALU.mult), reads=[yt[b], zt[b]], writes=[yt[b]])
            p.op("scalar", lambda e: e.activation(out=junk[:], in_=yt[b][:], func=AF.Square, accum_out=gss[b][:]),
                 reads=[yt[b]], writes=[junk, gss[b]])
            p.op("vector", lambda e: e.tensor_scalar(gss[b][:], gss[b][:], 1.0 / 1024, EPS, ALU.mult, ALU.add),
                 reads=[gss[b]], writes=[gss[b]])
            p.op("scalar", lambda e: e.sqrt(gss[b][:], gss[b][:]), reads=[gss[b]], writes=[gss[b]])
            p.op("vector", lambda e: e.reciprocal(gss[b][:], gss[b][:]), reads=[gss[b]], writes=[gss[b]])
            p.op("vector", lambda e: e.scalar_tensor_tensor(out=gb[b][:], in0=yt[b][:], scalar=gss[b][:], in1=gnw[:],
                                                            op0=ALU.mult, op1=ALU.mult), reads=[yt[b], gss[b], gnw], writes=[gb[b]])
            tpb = banks[6 + b]
            tp = tpb[:].bitcast(BF16).rearrange("p (j n) -> p j n", j=8)
            for j in range(8):
                p.op("tensor", lambda e, j=j: e.transpose(tp[:, j, :], gb[b][:, j * 128:(j + 1) * 128], ident[:]),
                     reads=[gb[b], ident], writes=[tpb])
            p.op("vector", lambda e: e.tensor_copy(out=ot[:, 8:16, :], in_=tp[:, :, :]), reads=[tpb], writes=[ot])
        p.dma("sync", hres[b][:], h[tt * 128:(tt + 1) * 128, :], reads=[h], writes=[hres[b]])
        for dq in range(4):
            bk = banks[(tt * 4 + dq) % 6]
            for kc in range(KC):
                p.op("tensor", lambda e, kc=kc: e.matmul(bk[:, :], ot[:, kc, :], Wsb[:, kc, dq * 512:(dq + 1) * 512],
                                                         start=(kc == 0), stop=(kc == KC - 1)), reads=[ot, Wsb], writes=[bk])
            p.op("vector", lambda e, dq=dq: e.tensor_tensor(stg[b][:, dq * 512:(dq + 1) * 512], bk[:, :],
                                                            hres[b][:, dq * 512:(dq + 1) * 512], ALU.add),
                 reads=[bk, hres[b]], writes=[stg[b]])
        p.dma("sync", out[tt * 128:(tt + 1) * 128, :], stg[b][:], reads=[stg[b]], writes=[out])
    p.end_phase()


def final_norm_phase(p, h, nwbc_ap, out):
    p.begin_phase()
    wbc = p.sbuf([128, D], F32, "fwbc")
    p.dma("sync", wbc[:], nwbc_ap, writes=[wbc])
    hb = [p.sbuf([128, D], F32, "fh") for _ in range(2)]
    ob = [p.sbuf([128, D], F32, "fo") for _ in range(2)]
    junk = p.sbuf([128, D], BF16, "fjunk")
    ss = [p.sbuf([128, 1], F32, "fss") for _ in range(2)]
    for tt in range(TPC // 128):
        b = tt % 2
        p.dma("sync", hb[b][:], h[tt * 128:(tt + 1) * 128, :], reads=[h], writes=[hb[b]])
        p.op("scalar", lambda e: e.activation(out=junk[:], in_=hb[b][:], func=AF.Square, accum_out=ss[b][:]),
             reads=[hb[b]], writes=[junk, ss[b]])
        p.op("vector", lambda e: e.tensor_scalar(ss[b][:], ss[b][:], 1.0 / D, EPS, ALU.mult, ALU.add), reads=[ss[b]], writes=[ss[b]])
        p.op("scalar", lambda e: e.sqrt(ss[b][:], ss[b][:]), reads=[ss[b]], writes=[ss[b]])
        p.op("vector", lambda e: e.reciprocal(ss[b][:], ss[b][:]), reads=[ss[b]], writes=[ss[b]])
        p.op("vector", lambda e: e.scalar_tensor_tensor(out=ob[b][:], in0=hb[b][:], scalar=ss[b][:], in1=wbc[:],
                                                        op0=ALU.mult, op1=ALU.mult), reads=[hb[b], ss[b], wbc], writes=[ob[b]])
        p.dma("sync", out[tt * 128:(tt + 1) * 128, :], ob[b][:], reads=[ob[b]], writes=[out])
    p.end_phase()


G4 = [[0, 1, 2, 3], [4, 5, 6, 7]]
DIN_E = 5648
DIN_O = 6144


class _Src:
    def __init__(self, fn, deps):
        self.fn = fn
        self.deps = deps

    def __call__(self, *a):
        return self.fn(*a)


def build_fused(depth=4, dump=None, stop=None):
    p = Prog()
    nc = p.nc
    ext = lambda name, shape, dt=F32: p.dram(name, shape, dt, "ExternalInput")
    x = ext("x", [TPC, D])
    mixn = [ext("mixn%d" % l, [128, D]) for l in range(4)]
    ffnn = [ext("ffnn%d" % l, [128, D]) for l in range(4)]
    finn = ext("finn", [128, D])
    flag = ext("flag", [128, 1])
    sh_ev_in = [ext("ev_in%d" % i, [D // 8, DIN_E]) for i in range(2)]
    sh_ev_out = [ext("ev_out%d" % i, [D // 8, D]) for i in range(2)]
    sh_od_in = [ext("od_in%d" % i, [D // 8, DIN_O]) for i in range(2)]
    sh_od_out = [ext("od_out%d" % i, [D // 8, D]) for i in range(2)]
    sh_wg = [ext("wg%d" % l, [D // 8, DFF]) for l in range(4)]
    sh_wu = [ext("wu%d" % l, [D // 8, DFF]) for l in range(4)]
    sh_wd = [ext("wd%d" % l, [DFF // 8, D]) for l in range(4)]
    ffn_cw = [ext("ffn_cw%d" % l, [128, DFF // 128, 4]) for l in range(4)]
    ev_cw = [ext("ev_cw%d" % i, [128, 4, 5]) for i in range(2)]
    ev_dtb = [ext("ev_dtb%d" % i, [128, 4]) for i in range(2)]
    ev_alog = [ext("ev_alog%d" % i, [128, 4]) for i in range(2)]
    ev_dsk = [ext("ev_dsk%d" % i, [128, 256]) for i in range(2)]
    ev_gnw = [ext("ev_gnw%d" % i, [128, 1024]) for i in range(2)]
    out = p.dram("out", [TPC, D], F32, "ExternalOutput")

    banks = alloc_banks(p)
    ident = make_identity(p)

    def prep(shards, rows, cols, name):
        res = []
        for i, sh in enumerate(shards):
            full = p.dram("%s_full%d" % (name, i), [rows * 8, cols], BF16)
            weight_prep(p, sh, full, rows, cols)
            res.append(full)
        return res
    n_even = (depth + 1) // 2
    n_odd = depth // 2
    W_ev_in = prep(sh_ev_in[:n_even], D // 8, DIN_E, "ev_in")
    W_ev_out = prep(sh_ev_out[:n_even], D // 8, D, "ev_out")
    W_od_in = prep(sh_od_in[:n_odd], D // 8, DIN_O, "od_in")
    W_od_out = prep(sh_od_out[:n_odd], D // 8, D, "od_out")
    W_g = prep(sh_wg[:depth], D // 8, DFF, "wg")
    W_u = prep(sh_wu[:depth], D // 8, DFF, "wu")
    W_d = prep(sh_wd[:depth], DFF // 8, D, "wd")

    hbuf = [p.dram("hA", [TPC, D], F32), p.dram("hB", [TPC, D], F32)]
    hm = p.dram("hm", [TPC, D], F32)
    halo_s = p.dram("halo_s", [2, D], F32)
    halo_g = p.dram("halo_g", [16, D], F32)

    def gathered(name, rows, cols, dt):
        loc = p.dram(name, [rows, cols], dt)
        gpad = p.dram(name + "g", [9 * rows, cols], dt)
        gat = Buf(gpad.t[0:8 * rows, :], name + "g")
        return loc, gat, gpad
    qTe, qTeg, qTegp = gathered("qTe", 1024, TPC, BF16)
    kTe, kTeg, kTegp = gathered("kTe", 1024, TPC, BF16)
    ve, veg, _ = gathered("ve", TPC, 1024, BF16)
    ze = p.dram("ze", [TPC, 1024], F32)
    xbcT, xbcTg, xbcTgp = gathered("xbcT", 1536, TPC, F32)
    dte, dteg, _ = gathered("dte", TPC, 16, F32)
    oTe, oTeg, _ = gathered("oTe", 256, SEQ, BF16)
    ye, yeg, _ = gathered("ye", SEQ, 256, F32)
    qTo, qTog, qTogp = gathered("qTo", 2048, TPC, BF16)
    kTo, kTog, kTogp = gathered("kTo", 2048, TPC, BF16)
    vo, vog, _ = gathered("vo", TPC, 2048, BF16)
    oTo, oTog, _ = gathered("oTo", 512, SEQ, BF16)
    qsel = p.dram("qsel", [256, SEQ], BF16); ksel = p.dram("ksel", [256, SEQ], BF16)
    vsel = p.dram("vsel", [SEQ, 256], BF16)
    xsel = p.dram("xsel", [512, SEQ], F32); dtsel = p.dram("dtsel", [SEQ, 4], F32)
    osel = p.dram("osel", [1024, TPC], BF16); ysel = p.dram("ysel", [TPC, 1024], F32)
    qselo = p.dram("qselo", [512, SEQ], BF16); kselo = p.dram("kselo", [512, SEQ], BF16)
    vselo = p.dram("vselo", [SEQ, 512], BF16); oselo = p.dram("oselo", [2048, TPC], BF16)

    ds = bass.ds
    pS = nc.sync.partition_id()
    jS, bS = pS % 4, pS // 4
    pA = nc.scalar.partition_id()
    jA, bA = pA % 4, pA // 4

    def selT(iss, bv, gat, gpad, dst, F, f0, nf, dst_r0=0):
        win = gpad.t[ds(bv * (4 * F) + f0, 4 * F), :].rearrange("(r f) t -> f r t", r=4)[0:nf, :, :]
        dv = dst.t[dst_r0:dst_r0 + nf, :].rearrange("f (r t) -> f r t", r=4)
        p.dma(iss, dv, win, reads=[gat], writes=[dst])

    def _stop(tag, buf=None):
        if stop == tag:
            if buf is not None:
                p.dma("sync", out[0:buf.t.shape[0] if buf.t.shape[0] < TPC else TPC, :] if False else out[:, :], buf[:, :], reads=[buf], writes=[out])
            return True
        return False

    h = x
    if _stop("prep"):
        return p.finish(), p
    for l in range(depth):
        i = l // 2
        if l % 2 == 0:
            specs = [(0, 1024, 'T', qTe, BF16), (1024, 1024, 'T', kTe, BF16), (2048, 1024, 'N', ve, BF16),
                     (3072, 1024, 'N', ze, F32), (4096, 1536, 'T', xbcT, F32), (5632, 16, 'N', dte, F32)]
            proj_phase(p, banks, ident, h, mixn[l].t, W_ev_in[i], specs)
            if _stop("proj0"):
                return p.finish(), p
            for s_, g_ in ((qTe, qTeg), (kTe, kTeg), (ve, veg), (xbcT, xbcTg), (dte, dteg)):
                p.allgather(s_, g_)
            if _stop("ag0"):
                return p.finish(), p
            selT("sync", bS, qTeg, qTegp, qsel, 1024, jS * 256, 256)
            selT("sync", bS, kTeg, kTegp, ksel, 1024, jS * 256, 256)
            p.dma("sync", vsel[:, :], veg[ds(bS * SEQ, SEQ), ds(jS * 256, 256)], reads=[veg], writes=[vsel])
            selT("sync", bS, xbcTg, xbcTgp, xsel, 1536, jS * 256, 256, 0)
            selT("sync", bS, xbcTg, xbcTgp, xsel, 1536, 1024 + (jS // 2) * 128, 128, 256)
            selT("sync", bS, xbcTg, xbcTgp, xsel, 1536, 1280 + (jS // 2) * 128, 128, 384)
            p.dma("sync", dtsel[:, :], dteg[ds(bS * SEQ, SEQ), ds(jS * 4, 4)], reads=[dteg], writes=[dtsel])
            if _stop("sel0"):
                return p.finish(), p
            sb_attn_phase(p, banks, qsel, ksel, vsel, oTe, NU=2, T=SEQ)
            p.allgather(oTe, oTeg)
            if _stop("sb0"):
                return p.finish(), p
            ssd_phase(p, banks, xsel, dtsel.t.rearrange("(c q) h -> q c h", q=128), ev_cw[i].t, ev_dtb[i].t, ev_alog[i].t,
                      ev_dsk[i].t, ye, T=SEQ)
            p.allgather(ye, yeg)
            p.dma("sync", osel[:, :], oTeg[ds(bS * 1024, 1024), ds(jS * TPC, TPC)], reads=[oTeg], writes=[osel])
            p.dma("sync", ysel.t.rearrange("t (hq c) -> t hq c", hq=4),
                  yeg.t.rearrange("(hq t) c -> t hq c", hq=8)[ds(jS * TPC, TPC), ds(bS * 4, 4), :], reads=[yeg], writes=[ysel])
            o_src = _Src(lambda kc, tt: osel[kc * 128:(kc + 1) * 128, tt * 128:(tt + 1) * 128], [osel])
            y_src = lambda hq, tt: ysel[tt * 128:(tt + 1) * 128, hq * 256:(hq + 1) * 256]
            outproj_phase(p, banks, ident, h, o_src, 8, W_ev_out[i], hm, gate=(y_src, ze, ev_gnw[i].t, [ysel]))
        else:
            specs = [(0, 2048, 'T', qTo, BF16), (2048, 2048, 'T', kTo, BF16), (4096, 2048, 'N', vo, BF16)]
            proj_phase(p, banks, ident, h, mixn[l].t, W_od_in[i], specs)
            for s_, g_ in ((qTo, qTog), (kTo, kTog), (vo, vog)):
                p.allgather(s_, g_)
            selT("scalar", bA, qTog, qTogp, qselo, 2048, jA * 512, 512)
            selT("scalar", bA, kTog, kTogp, kselo, 2048, jA * 512, 512)
            p.dma("scalar", vselo[:, :], vog[ds(bA * SEQ, SEQ), ds(jA * 512, 512)], reads=[vog], writes=[vselo])
            dilated_phase(p, banks, qselo, kselo, vselo, oTo, NU=4, T=SEQ)
            p.allgather(oTo, oTog)
            p.dma("scalar", oselo[:, :], oTog[ds(bA * 2048, 2048), ds(jA * TPC, TPC)], reads=[oTog], writes=[oselo])
            o_src = _Src(lambda kc, tt: oselo[kc * 128:(kc + 1) * 128, tt * 128:(tt + 1) * 128], [oselo])
            outproj_phase(p, banks, ident, h, o_src, 16, W_od_out[i], hm)
        p.dma("sync", halo_s[:, :], hm[TPC - 2:TPC, :], reads=[hm], writes=[halo_s])
        p.allgather(halo_s, halo_g)
        hn = hbuf[l % 2]
        ffn_phase(p, banks, ident, hm, halo_g, halo_g[ds((bS * 4 + (jS + 3) % 4) * 2, 2), :], ffnn[l].t,
                  W_g[l], W_u[l], ffn_cw[l].t, W_d[l], hn, flag_ap=flag.t)
        h = hn
        if dump is not None and dump.get("after_layer") == l:
            break
    if dump is None:
        final_norm_phase(p, h, finn.t, out)
    else:
        p.dma("sync", out[:, :], h[:, :], reads=[h], writes=[out])
    return p.finish(), p


def _bc(v, n=128):
    return np.ascontiguousarray(np.broadcast_to(np.asarray(v, np.float32), (n,) + np.asarray(v).shape))


def make_in_maps(inp, depth=4):
    f = lambda a: np.ascontiguousarray(np.asarray(a, dtype=np.float32))
    xs = f(inp["x"]).reshape(NTOK, D)
    common = {}
    for l in range(4):
        common["mixn%d" % l] = _bc(inp["mix_norm_w"][l])
        common["ffnn%d" % l] = _bc(inp["ffn_norm_w"][l])
        cw = np.concatenate([f(inp["ffn_conv_w"][l]), f(inp["ffn_conv_b"][l])[None]], 0)
        common["ffn_cw%d" % l] = np.ascontiguousarray(cw.reshape(4, DFF // 128, 128).transpose(2, 1, 0))
    common["finn"] = _bc(inp["final_norm_w"])
    for i in range(2):
        cw = np.concatenate([f(inp["ev_conv_w"][i]), f(inp["ev_conv_b"][i])[None]], 0)
        cwl = np.ascontiguousarray(cw.reshape(5, 12, 128).transpose(2, 1, 0))
        for jj in range(4):
            tiles = [2 * jj, 2 * jj + 1, 8 + jj // 2, 10 + jj // 2]
            common["ev_cw%d_j%d" % (i, jj)] = np.ascontiguousarray(cwl[:, tiles, :])
            common["ev_dtb%d_j%d" % (i, jj)] = _bc(f(inp["ev_dt_bias"][i])[4 * jj:4 * jj + 4])
            common["ev_alog%d_j%d" % (i, jj)] = _bc(f(inp["ev_a_log"][i])[4 * jj:4 * jj + 4])
            common["ev_dsk%d_j%d" % (i, jj)] = _bc(np.repeat(f(inp["ev_d_skip"][i])[4 * jj:4 * jj + 4], 64))
        common["ev_gnw%d" % i] = _bc(inp["ev_ssm_norm_w"][i])
    maps = []
    for c in range(NCORES):
        m = {k_: v_ for k_, v_ in common.items() if "_j" not in k_}
        for i in range(2):
            for nm in ("ev_cw", "ev_dtb", "ev_alog", "ev_dsk"):
                m["%s%d" % (nm, i)] = common["%s%d_j%d" % (nm, i, c % 4)]
        m["x"] = xs[c * TPC:(c + 1) * TPC]
        m["flag"] = np.full((128, 1), 0.0 if c % 4 == 0 else 1.0, np.float32)
        r = slice(c * (D // 8), (c + 1) * (D // 8))
        rd = slice(c * (DFF // 8), (c + 1) * (DFF // 8))
        for i in range(2):
            m["ev_in%d" % i] = f(inp["ev_w_in"][i][r])
            m["ev_out%d" % i] = f(inp["ev_w_out"][i][r])
            m["od_in%d" % i] = f(inp["od_w_in"][i][r])
            m["od_out%d" % i] = f(inp["od_w_out"][i][r])
        for l in range(4):
            m["wg%d" % l] = f(inp["ffn_w_gate"][l][r])
            m["wu%d" % l] = f(inp["ffn_w_up"][l][r])
            m["wd%d" % l] = f(inp["ffn_w_down"][l][rd])
        maps.append(m)
    return maps


def kernel(**inputs):
    nc, _ = build_fused()
    maps = make_in_maps(inputs)
    res = run_bass_kernel_spmd(nc, maps, core_ids=list(range(NCORES)))
    outp = np.concatenate([res.results[c]["out"] for c in range(NCORES)], axis=0)
    return outp.reshape(BATCH, SEQ, D).astype(np.float32)
```
